# Optimizing a Trainium2 kernel written in Bass

```python
import math
import jax, jax.numpy as jnp
from jax import lax
import numpy as np

D_MODEL = 1024
BATCH = 8
SEQ = 4096
DEPTH = 4

MIX_WIDTH = D_MODEL
ATTN_HEADS = 8
ATTN_HEAD_DIM = 64
ATTN_DIM = ATTN_HEADS * ATTN_HEAD_DIM
CONV_DIM = D_MODEL // 4
CONV_WIDTH = 3
POOL_WINDOWS = (2, 4, 8, 16)
N_POOL_GROUPS = len(POOL_WINDOWS)
POOL_DIM = D_MODEL // 4
POOL_GD = POOL_DIM // N_POOL_GROUPS
D_FF = 2816
Q_BLOCK = 128
RMS_EPS = 1e-6
SPLIT_SIZES = (ATTN_DIM, ATTN_DIM, ATTN_DIM, CONV_DIM, CONV_DIM, CONV_DIM, POOL_DIM)
IN_PROJ_WIDTH = sum(SPLIT_SIZES)
SPLIT_POINTS = tuple(int(v) for v in np.cumsum(SPLIT_SIZES)[:-1])

kernel_name = "hybrid_sbattn_shortconv_pool_macaron"


def _rmsnorm(x, g):
    x32 = x.astype(jnp.float32)
    y = x32 * lax.rsqrt(jnp.mean(x32 * x32, axis=-1, keepdims=True) + RMS_EPS)
    return (y * g.astype(jnp.float32)).astype(x.dtype)


def _swiglu(h, w_gate, w_up, w_down):
    return (jax.nn.silu(h @ w_gate) * (h @ w_up)) @ w_down


def _stick_breaking_attention(q, k, v):
    B, S, H, dh = q.shape
    nb = S // Q_BLOCK
    scale = 1.0 / math.sqrt(dh)
    kh = k.transpose(0, 2, 1, 3)
    vh = v.transpose(0, 2, 1, 3)
    q_blocks = q.transpose(0, 2, 1, 3).reshape(B, H, nb, Q_BLOCK, dh).transpose(2, 0, 1, 3, 4)
    k_pos = jnp.arange(S)

    def block(args):
        q_blk, start = args
        q_pos = start + jnp.arange(Q_BLOCK)
        z = jnp.einsum('bhqd,bhkd->bhqk', q_blk, kh).astype(jnp.float32) * scale
        mask = k_pos[None, :] < q_pos[:, None]
        log_keep = jnp.where(mask, jax.nn.log_sigmoid(-z), 0.0)
        between = lax.cumsum(log_keep, axis=3, reverse=True) - log_keep
        a = jnp.where(mask, jnp.exp(jax.nn.log_sigmoid(z) + between), 0.0)
        return jnp.einsum('bhqk,bhkd->bhqd', a.astype(vh.dtype), vh)

    out = lax.map(block, (q_blocks, jnp.arange(nb) * Q_BLOCK))
    return out.transpose(1, 0, 3, 2, 4).reshape(B, S, H * dh)


def _short_conv(gate_b, gate_c, h, conv_w):
    S = h.shape[1]
    u = gate_c * h
    u_pad = jnp.pad(u, ((0, 0), (CONV_WIDTH - 1, 0), (0, 0)))
    y = conv_w[0] * u_pad[:, 0:S] + conv_w[1] * u_pad[:, 1:S + 1] + conv_w[2] * u_pad[:, 2:S + 2]
    return gate_b * y


def _multiscale_pool(p, pool_w, pool_scale):
    B, S, C = p.shape
    p32 = p.astype(jnp.float32).reshape(B, S, N_POOL_GROUPS, POOL_GD)
    cs0 = jnp.pad(jnp.cumsum(p32, axis=1), ((0, 0), (1, 0), (0, 0), (0, 0)))
    count_base = jnp.arange(1, S + 1, dtype=jnp.float32)
    outs = []
    for gi, w in enumerate(POOL_WINDOWS):
        lagged = jnp.pad(cs0[:, :S + 1 - w, gi], ((0, 0), (w - 1, 0), (0, 0)))
        mean = (cs0[:, 1:, gi] - lagged) / jnp.minimum(count_base, w)[None, :, None]
        outs.append(mean - p32[:, :, gi])
    d = jnp.stack(outs, axis=2).astype(p.dtype)
    y = jnp.einsum('bsgc,gcd->bsgd', d, pool_w).reshape(B, S, C)
    return y * pool_scale


def _mixer(h, w_in, conv_w, pool_w, pool_scale, w_out):
    B, S, _ = h.shape
    proj = h @ w_in
    q, k, v, gate_b, gate_c, conv_h, pool_in = jnp.split(proj, SPLIT_POINTS, axis=-1)
    q = q.reshape(B, S, ATTN_HEADS, ATTN_HEAD_DIM)
    k = k.reshape(B, S, ATTN_HEADS, ATTN_HEAD_DIM)
    v = v.reshape(B, S, ATTN_HEADS, ATTN_HEAD_DIM)
    attn = _stick_breaking_attention(q, k, v)
    conv = _short_conv(gate_b, gate_c, conv_h, conv_w)
    pool = _multiscale_pool(pool_in, pool_w, pool_scale)
    return jnp.concatenate([attn, conv, pool], axis=-1) @ w_out


def setup_inputs(seed: int = 0) -> dict:
    key = jax.random.key(seed)
    ks = jax.random.split(key, 11)
    f32 = jnp.float32
    x = jax.random.normal(ks[0], (BATCH, SEQ, D_MODEL), f32)
    norm_g = 1.0 + 0.05 * jax.random.normal(ks[1], (DEPTH, 6, D_MODEL), f32)
    ffn_w_gate = jax.random.normal(ks[2], (DEPTH, 2, D_MODEL, D_FF), f32) * D_MODEL ** -0.5
    ffn_w_up = jax.random.normal(ks[3], (DEPTH, 2, D_MODEL, D_FF), f32) * D_MODEL ** -0.5
    ffn_w_down = jax.random.normal(ks[4], (DEPTH, 2, D_FF, D_MODEL), f32) * D_FF ** -0.5
    w_in = jax.random.normal(ks[5], (DEPTH, D_MODEL, IN_PROJ_WIDTH), f32) * D_MODEL ** -0.5
    conv_w = jax.random.normal(ks[6], (DEPTH, CONV_WIDTH, CONV_DIM), f32) * CONV_WIDTH ** -0.5
    pool_w = jax.random.normal(ks[7], (DEPTH, N_POOL_GROUPS, POOL_GD, POOL_GD), f32) * POOL_GD ** -0.5
    pool_scale = 1.0 + 0.05 * jax.random.normal(ks[8], (DEPTH, POOL_DIM), f32)
    w_out = jax.random.normal(ks[9], (DEPTH, MIX_WIDTH, D_MODEL), f32) * MIX_WIDTH ** -0.5
    return {"x": x, "norm_g": norm_g, "ffn_w_gate": ffn_w_gate, "ffn_w_up": ffn_w_up,
            "ffn_w_down": ffn_w_down, "w_in": w_in, "conv_w": conv_w, "pool_w": pool_w,
            "pool_scale": pool_scale, "w_out": w_out}


def reference(x, norm_g, ffn_w_gate, ffn_w_up, ffn_w_down, w_in, conv_w, pool_w, pool_scale, w_out):
    for l in range(DEPTH):
        g = norm_g[l]
        f = _swiglu(_rmsnorm(x, g[0]), ffn_w_gate[l, 0], ffn_w_up[l, 0], ffn_w_down[l, 0])
        x = x + 0.5 * _rmsnorm(f, g[1])
        m = _mixer(_rmsnorm(x, g[2]), w_in[l], conv_w[l], pool_w[l], pool_scale[l], w_out[l])
        x = x + _rmsnorm(m, g[3])
        f = _swiglu(_rmsnorm(x, g[4]), ffn_w_gate[l, 1], ffn_w_up[l, 1], ffn_w_down[l, 1])
        x = x + 0.5 * _rmsnorm(f, g[5])
    return x
```

```python
import contextlib
import numpy as np
import concourse.bass as bass
import concourse.mybir as mybir
from concourse.bass_utils import run_bass_kernel_spmd

F32 = mybir.dt.float32
BF16 = mybir.dt.bfloat16
AF = mybir.ActivationFunctionType
ALU = mybir.AluOpType

NCORES = 8
S = 4096
D = 1024
DFF = 2816
NF = 22
DEPTH = 4
NT = S // 128
EPS = 1e-6
NSLOT = 8
CH_PER_LAYER = 160
POOL_W = (2, 4, 8, 16)


class Buf:
    __slots__ = ("name", "w", "r")

    def __init__(self, name):
        self.name = name
        self.w = None
        self.r = []


class Op:
    __slots__ = ("eng", "fn", "deps", "is_dma", "key", "sig", "has_dep")

    def __init__(self, eng, fn, is_dma, key):
        self.eng = eng
        self.fn = fn
        self.deps = []
        self.is_dma = is_dma
        self.key = key
        self.sig = None
        self.has_dep = False


class Sched:
    ENGS = ("pe", "act", "dve", "pool", "sp")

    def __init__(self):
        self.q = {e: [] for e in self.ENGS}
        self.dma_keys = []
        self.last_dma = {}
        self.pending_barrier = {e: None for e in self.ENGS}

    def add(self, eng, fn, reads=(), writes=(), dma=None):
        op = Op(eng, fn, dma is not None, dma)
        if dma is not None:
            if dma not in self.dma_keys:
                self.dma_keys.append(dma)
            self.last_dma[dma] = op
        deps = []
        for b in reads:
            if b.w is not None:
                deps.append((b.w, "raw"))
        for b in writes:
            if b.w is not None:
                deps.append((b.w, "waw"))
            for r in b.r:
                deps.append((r, "war"))
        seen = set()
        for d, kind in deps:
            if d is op or id(d) in seen:
                continue
            if (not d.is_dma) and (not op.is_dma) and d.eng == eng:
                if eng == "pe" or kind != "raw":
                    continue
            seen.add(id(d))
            op.deps.append(d)
            d.has_dep = True
        pb = self.pending_barrier[eng]
        if pb is not None:
            for d in pb:
                if id(d) not in seen and d is not op:
                    seen.add(id(d))
                    op.deps.append(d)
                    d.has_dep = True
            self.pending_barrier[eng] = None
        for b in writes:
            b.w = op
            b.r = []
        for b in reads:
            if b.w is not op:
                b.r.append(op)
        self.q[eng].append(op)
        return op

    def barrier(self):
        ops = []
        for e in self.ENGS:
            for op in reversed(self.q[e]):
                if not op.is_dma:
                    ops.append(op)
                    break
        ops.extend(self.last_dma.values())
        for e in self.ENGS:
            prev = self.pending_barrier[e]
            self.pending_barrier[e] = list(ops) if prev is None else prev + ops

    def emit(self, nc, st):
        esem = {e: st.enter_context(nc.semaphore("s_" + e)) for e in ("pe", "act", "dve", "pool", "sp")}
        dsem = {k: st.enter_context(nc.semaphore("d_" + k)) for k in self.dma_keys}
        finals = list(self.last_dma.values())
        cnt = {e: 0 for e in esem}
        dcnt = {k: 0 for k in dsem}
        for e in self.ENGS:
            for op in self.q[e]:
                if op.is_dma:
                    dcnt[op.key] += 16
                    op.sig = (dsem[op.key], dcnt[op.key])
                elif op.has_dep:
                    cnt[e] += 1
                    op.sig = (esem[e], cnt[e])
        block = st.enter_context(nc.Block())
        engmap = {"pe": "tensor", "act": "scalar", "dve": "vector", "pool": "gpsimd", "sp": "sync"}

        def make(e):
            def body(engine):
                known = {}
                for op in self.q[e]:
                    for d in op.deps:
                        sem, val = d.sig
                        kk = id(sem)
                        if known.get(kk, 0) >= val:
                            continue
                        engine.wait_ge(sem, val)
                        known[kk] = val
                    name_, args_, kw_ = op.fn
                    ins = getattr(engine, name_)(*args_, **kw_)
                    if op.sig is not None:
                        ins.then_inc(op.sig[0], 16 if op.is_dma else 1)
                if e == "sp":
                    for op in finals:
                        sem, val = op.sig
                        if known.get(id(sem), 0) < val:
                            engine.wait_ge(sem, val)
            return body

        for e in self.ENGS:
            getattr(block, engmap[e])(make(e))


def I(name, *args, **kwargs):
    return (name, args, kwargs)


def make_consts():
    c = np.zeros((128, 548), np.float32)
    p = np.arange(128)[:, None]
    f = np.arange(128)[None, :]
    c[:, 0:128] = np.eye(128, dtype=np.float32)
    c[:, 128:256] = np.where(p >= f, -1.0, 0.0)
    c[:, 256:384] = -1.0
    c[:, 384:512] = np.where(p >= f, -30000.0, 0.0)
    for ch in range(2):
        for pp in range(128):
            w = POOL_W[2 * ch + pp // 64]
            for t in range(16):
                c[pp, 512 + ch * 16 + t] = 1.0 / min(t + 1, w)
            c[pp, 544 + ch] = 1.0 / w
    c[:, 546] = -0.5
    return c


def chunk_id(kind, l, j, idx):
    base = l * CH_PER_LAYER
    if kind == "g":
        return base + j * 66 + idx
    if kind == "u":
        return base + j * 66 + 22 + idx
    if kind == "d":
        return base + j * 66 + 44 + idx
    if kind == "wi":
        return base + 132 + idx
    if kind == "wo":
        return base + 152 + idx
    raise ValueError(kind)


def phase_list(nphases):
    ph = []
    for l in range(DEPTH):
        ph.append(("ffn", l, 0))
        ph.append(("mix", l, 0))
        ph.append(("ffn", l, 1))
    return ph[:nphases]


def ring_order(phases):
    order = []
    for kind, l, j in phases:
        if kind == "ffn":
            for G in range(4):
                for f in range(NF):
                    order.append(("g", l, j, f))
                    order.append(("u", l, j, f))
        else:
            for i in range(8):
                for ch in range(20):
                    order.append(("wi", l, 0, ch))
                for tt in range(4):
                    for kc in range(8):
                        order.append(("wo", l, 0, kc))
    return order


def build_program(nphases=12):
    rec = _build(nphases, None)
    return _build(nphases, rec)


def _build(nphases, order_in):
    nc = bass.Bass("TRN2", target_bir_lowering=False)
    phases = phase_list(nphases)
    layers_needed = sorted({l for _, l, _ in phases})

    def din(name, shape):
        return nc.dram_tensor(name, shape, F32, kind="ExternalInput").ap()

    x_in = din("x", [S, D])
    norm_g = din("norm_g", [DEPTH, 6, D])
    w_gate = din("ffn_w_gate", [DEPTH, 2, D, DFF])
    w_up = din("ffn_w_up", [DEPTH, 2, D, DFF])
    w_down = din("ffn_w_down", [DEPTH, 2, DFF, D])
    w_in = din("w_in", [DEPTH, D, 2560])
    conv_w = din("conv_w", [DEPTH, 3, 256])
    pool_w = din("pool_w", [DEPTH, 4, 64, 64])
    pool_scale = din("pool_scale", [DEPTH, 256])
    w_out = din("w_out", [DEPTH, D, D])
    consts = din("consts", [128, 548])
    y_out = nc.dram_tensor("y", [S, D], F32, kind="ExternalOutput").ap()
    xbuf = nc.dram_tensor("xbuf", [S, D], F32, kind="Internal").ap()
    wsc = nc.dram_tensor("wsc", [128, DEPTH * CH_PER_LAYER, 1024], BF16, kind="Internal").ap()

    sch = Sched()
    st = contextlib.ExitStack()
    with st:
        def sb(name, shape, dt):
            return st.enter_context(nc.sbuf_tensor(name, shape, dt))

        cst = sb("cst", [128, 548], F32)
        ident_b = sb("ident_b", [128, 128], BF16)
        uneg_b = sb("uneg_b", [128, 128], BF16)
        nones_b = sb("nones_b", [128, 128], BF16)
        nmask_b = sb("nmask_b", [128, 128], BF16)
        paramT = sb("paramT", [128, 224], F32)
        BD = sb("BD", [128, 8, 128], BF16)
        ring = sb("ring", [128, NSLOT, 1024], BF16)
        xld = sb("xld", [128, 4, 1024], F32)
        gbc = sb("gbc", [128, 2, 1024], F32)
        junk = sb("junk", [128, 1024], BF16)
        stats = sb("stats", [128, 64], F32)
        xs = sb("xs", [128, 2, 1024], BF16)
        OVL = 148 * 1024 // 2
        big = sb("big", [128, OVL], BF16)
        banks = [st.enter_context(nc.psum_tensor("bank%d" % i, [128, 512], F32)) for i in range(8)]
        bankb = [Buf("bank%d" % i) for i in range(8)]

        ident_f = cst[:, 0:128]

        class Carver:
            def __init__(self):
                self.off = 0

            def take(self, shape, dt):
                n = 1
                for s_ in shape[1:]:
                    n *= s_
                nb = n * (4 if dt == F32 else 2)
                nb = (nb + 63) // 64 * 64
                o = self.off
                self.off += nb
                assert self.off <= OVL * 2, ("overlay overflow", self.off)
                v = big[:, o // 2:(o + nb) // 2]
                if dt == F32:
                    v = v.bitcast(F32)
                v = v[:, 0:n]
                if len(shape) == 3:
                    v = v.rearrange("p (a b) -> p a b", a=shape[1])
                elif len(shape) == 4:
                    v = v.rearrange("p (a b c) -> p a b c", a=shape[1], b=shape[2])
                return v

        cv = Carver()
        stg_f = [cv.take([128, 4096], F32) for _ in range(2)]
        stg_b = [cv.take([128, 4096], BF16) for _ in range(2)]
        pstage = cv.take([128, 2, 128], F32)
        bdstage = cv.take([128, 8, 128], F32)
        cv = Carver()
        hT = cv.take([128, NF, 1024], BF16)
        xnT_f = cv.take([128, 8, 1024], BF16)
        Wd = cv.take([128, NF, 1024], BF16)
        ytmp = cv.take([128, 2, 1024], F32)
        sg = cv.take([128, 2, 512], F32)
        jstg_f = [cv.take([128, 2048], F32) for _ in range(2)]
        jstg_b = [cv.take([128, 2048], BF16) for _ in range(2)]
        cv = Carver()
        kT = cv.take([128, 4, S], BF16)
        vS = cv.take([128, NT, 512], BF16)
        xnT_m = cv.take([128, 8, 512], BF16)
        qT = cv.take([128, 4, 2, 512], BF16)
        mixT = cv.take([128, 8, 512], BF16)
        SPt = cv.take([128, 3, 512], BF16)
        At = cv.take([128, 3, 512], BF16)
        Rt = cv.take([128, 4, 512], BF16)
        Bsb = cv.take([128, 2, 512], F32)
        Csb = cv.take([128, 2, 512], F32)
        ubuf = cv.take([128, 2, 514], F32)
        ycv = cv.take([128, 2, 512], F32)
        pbuf = cv.take([128, 2, 528], F32)
        s2 = cv.take([128, 2, 528], F32)
        s4 = cv.take([128, 2, 528], F32)
        s8 = cv.take([128, 528], F32)
        s16 = cv.take([128, 528], F32)
        dT = cv.take([128, 2, 512], BF16)
        t16 = cv.take([128, 2, 16], F32)
        ytmp_m = cv.take([128, 2, 1024], F32)

        B = {}

        def buf(name):
            if name not in B:
                B[name] = Buf(name)
            return B[name]

        stat_ctr = [0]

        def stat():
            i = stat_ctr[0] % 64
            stat_ctr[0] += 1
            return stats[:, i:i + 1], buf("stat%d" % i)

        xld_ctr = [0]
        xs_ctr = [0]

        def xslot():
            i = xld_ctr[0] % 4
            xld_ctr[0] += 1
            return i, buf("xld%d" % i)

        sch.add("sp", I("dma_start", out=cst[:], in_=consts), writes=[buf("cst")], dma="misc0")
        for nm, t, c0 in (("ident_b", ident_b, 0), ("uneg_b", uneg_b, 128), ("nones_b", nones_b, 256), ("nmask_b", nmask_b, 384)):
            sch.add("dve", I("tensor_copy", out=t[:], in_=cst[:, c0:c0 + 128]),
                    reads=[buf("cst")], writes=[buf(nm)])
        ng = norm_g.rearrange("l j (c p) -> (l j c) p", p=128)
        cwv = conv_w.rearrange("l k (c p) -> (l k c) p", p=128)
        psv = pool_scale.rearrange("l (c p) -> (l c) p", p=128)
        sch.add("sp", I("dma_start", out=pstage[:, 0, :], in_=ng[0:128]), writes=[buf("pstage")], dma="misc1")
        sch.add("sp", I("dma_start", out=pstage[0:64, 1, :], in_=ng[128:192]), writes=[buf("pstage")], dma="misc2")
        sch.add("sp", I("dma_start", out=pstage[64:88, 1, :], in_=cwv), writes=[buf("pstage")], dma="misc3")
        sch.add("sp", I("dma_start", out=pstage[88:96, 1, :], in_=psv), writes=[buf("pstage")], dma="misc4")
        sch.add("pe", I("transpose", out=banks[0][:, 0:128], in_=pstage[:, 0, :], identity=ident_f),
                reads=[buf("pstage"), buf("cst")], writes=[bankb[0]])
        sch.add("pe", I("transpose", out=banks[0][:, 128:224], in_=pstage[0:96, 1, :], identity=cst[0:96, 0:96]),
                reads=[buf("pstage"), buf("cst")], writes=[bankb[0]])
        sch.add("dve", I("tensor_copy", out=paramT[:], in_=banks[0][:, 0:224]), reads=[bankb[0]], writes=[buf("paramT")])
        sch.add("pool", I("memset", bdstage[:], 0.0), writes=[buf("bdstage")])
        pwv = pool_w.rearrange("l (ch gp) c d -> gp c (l ch) d", gp=2)
        for gp in range(2):
            sch.add("sp", I("dma_start", out=bdstage[gp * 64:(gp + 1) * 64, :, gp * 64:(gp + 1) * 64], in_=pwv[gp]),
                    writes=[buf("bdstage")], dma="misc%d" % (5 + gp))
        sch.add("dve", I("tensor_copy", out=BD[:], in_=bdstage[:]), reads=[buf("bdstage")], writes=[buf("BD")])

        def phase_units(kind, l, j, step):
            us = []
            if kind == "ffn":
                for knd, W in (("g", w_gate), ("u", w_up)):
                    for u0 in range(0, NF, step):
                        n = min(step, NF - u0)
                        us.append(("col", W[l, j], u0, n, chunk_id(knd, l, j, u0)))
                for u0 in range(0, NF, step):
                    n = min(step, NF - u0)
                    us.append(("row", w_down[l, j], u0, n, chunk_id("d", l, j, u0)))
            else:
                for u0 in range(0, 20, step):
                    us.append(("col", w_in[l], u0, min(step, 20 - u0), chunk_id("wi", l, 0, u0)))
                for u0 in range(0, 8, step):
                    us.append(("row", w_out[l], u0, min(step, 8 - u0), chunk_id("wo", l, 0, u0)))
            return us

        units = phase_units(*phases[0], 4)
        wscb = {}
        cast_engs = ("dve", "act")
        def unit_views(ui):
            typ, W, u0, n, cid = units[ui]
            sf = stg_f[ui % 2]
            sbf = stg_b[ui % 2]
            if typ == "col":
                src = W[:, u0 * 128:(u0 + n) * 128].rearrange("(kc p) c -> p kc c", p=128)
                dstv = sf[:, 0:8 * n * 128].rearrange("p (kc c) -> p kc c", kc=8)
                cin = sf[:, 0:8 * n * 128].rearrange("p (kc j c) -> p j kc c", kc=8, j=n)
                cout = sbf[:, 0:n * 1024].rearrange("p (j kc c) -> p j kc c", j=n, kc=8)
            else:
                src = W[u0 * 128:(u0 + n) * 128, :].rearrange("(f p) c -> p f c", p=128)
                dstv = sf[:, 0:n * 1024].rearrange("p (f c) -> p f c", f=n)
                cin = sf[:, 0:n * 1024]
                cout = sbf[:, 0:n * 1024]
            return src, dstv, cin, cout, sbf, n, cid

        def unit_load(ui):
            src, dstv, cin, cout, sbf, n, cid = unit_views(ui)
            sch.add("sp", I("dma_start", out=dstv, in_=src), writes=[buf("stgf%d" % (ui % 2))], dma="pl%d" % (ui % 2))

        if units:
            unit_load(0)
        for ui in range(len(units)):
            if ui + 1 < len(units):
                unit_load(ui + 1)
            src, dstv, cin, cout, sbf, n, cid = unit_views(ui)
            bf_ = buf("stgf%d" % (ui % 2))
            bb_ = buf("stgb%d" % (ui % 2))
            ce = cast_engs[ui % 2]
            if ce == "act":
                sch.add("act", I("activation", out=cout, in_=cin, func=AF.Copy), reads=[bf_], writes=[bb_])
            else:
                sch.add(ce, I("tensor_copy", out=cout, in_=cin), reads=[bf_], writes=[bb_])
            wb = buf("wsc%d" % cid)
            for k in range(n):
                wscb[cid + k] = wb
            sch.add("sp", I("dma_start", out=wsc[:, cid:cid + n, :], in_=sbf[:, 0:n * 1024].rearrange("p (j c) -> p j c", j=n)),
                    reads=[bb_], writes=[wb], dma="ps%d" % (ui % 2))
        sch.barrier()

        order = order_in
        rec_order = []
        ring_pos = [0]
        ringb = [buf("ring%d" % s_) for s_ in range(NSLOT)]

        def ring_issue(k):
            if order is None or k >= len(order):
                return
            kind, l, j, idx = order[k]
            cid = chunk_id(kind, l, j, idx)
            s_ = k % NSLOT
            sch.add("sp", I("dma_start", out=ring[:, s_, :], in_=wsc[:, cid, :]),
                    reads=[wscb[cid]], writes=[ringb[s_]], dma="r%d" % s_)

        for k in range(NSLOT):
            ring_issue(k)

        def ring_acquire(key):
            k = ring_pos[0]
            rec_order.append(key)
            if order is not None:
                assert order[k] == key, (order[k], key)
            s_ = k % NSLOT
            return ring[:, s_, :], ringb[s_]

        def ring_release():
            k = ring_pos[0]
            ring_pos[0] += 1
            ring_issue(k + NSLOT)

        def gT_cols(l, j):
            o = (l * 6 + j) * 8
            return paramT[:, o:o + 8]

        def prenorm_tile(src, row0, l, jn, xnT, col0, tbank):
            prenorm_b(prenorm_a(src, row0, l, jn, xnT, col0, tbank))

        def prenorm_a(src, row0, l, jn, xnT, col0, tbank):
            si, sbuf_ = xslot()
            sch.add("sp", I("dma_start", out=xld[:, si, :], in_=src[row0:row0 + 128, :]),
                    reads=[buf("xd%d" % (row0 // 128))], writes=[sbuf_], dma="xl%d" % si)
            ssq, ssqb = stat()
            rms, rmsb = stat()
            rstd, rstdb = stat()
            sch.add("act", I("activation", out=junk[:], in_=xld[:, si, :], func=AF.Square, accum_out=ssq),
                    reads=[sbuf_], writes=[buf("junk"), ssqb])
            sch.add("pool", I("tensor_scalar", out=rms, in0=ssq, scalar1=1.0 / D, scalar2=EPS, op0=ALU.mult, op1=ALU.add),
                    reads=[ssqb], writes=[rmsb])
            sch.add("pool", I("tensor_tensor", out=rstd, in0=rms, in1=cst[:, 546:547], op=ALU.pow), reads=[rmsb, buf("cst")], writes=[rstdb])
            xs_ctr[0] += 1
            par = xs_ctr[0] % 2
            xsb = buf("xs%d" % par)
            sch.add("dve", I("tensor_scalar", out=xs[:, par, :], in0=xld[:, si, :], scalar1=rstd, scalar2=None, op0=ALU.mult),
                    reads=[sbuf_, rstdb], writes=[xsb])
            return (par, xsb, l, jn, xnT, col0, tbank)

        def prenorm_b(state):
            par, xsb, l, jn, xnT, col0, tbank = state
            pT = banks[tbank][:].bitcast(BF16)
            for c in range(8):
                sch.add("pe", I("transpose", out=pT[:, c * 128:(c + 1) * 128], in_=xs[:, par, c * 128:(c + 1) * 128],
                                                          identity=ident_b[:]),
                        reads=[xsb, buf("ident_b")], writes=[bankb[tbank]])
            g8 = gT_cols(l, jn)
            sch.add("dve", I("tensor_tensor", out=xnT[:, :, col0:col0 + 128], in0=pT.rearrange("p (c t) -> p c t", c=8),
                                                     in1=g8.unsqueeze(2).to_broadcast([128, 8, 128]), op=ALU.mult),
                    reads=[bankb[tbank], buf("paramT")], writes=[buf("xnT")])

        def postnorm_tile(pb0, pb1, bb0, bb1, src, dst, row0, gpar, half_scale, ytile, ytb, is_last_phase):
            si, sbuf_ = xslot()
            sch.add("sp", I("dma_start", out=xld[:, si, :], in_=src[row0:row0 + 128, :]),
                    reads=[buf("xd%d" % (row0 // 128))], writes=[sbuf_], dma="xl%d" % si)
            q0, q0b = stat()
            q1, q1b = stat()
            qs, qsb = stat()
            rms, rmsb = stat()
            rstd, rstdb = stat()
            sch.add("act", I("activation", out=junk[:, 0:512], in_=pb0[:], func=AF.Square, accum_out=q0),
                    reads=[bb0], writes=[buf("junk"), q0b])
            sch.add("act", I("activation", out=junk[:, 512:1024], in_=pb1[:], func=AF.Square, accum_out=q1),
                    reads=[bb1], writes=[buf("junk"), q1b])
            sch.add("pool", I("tensor_tensor", out=qs, in0=q0, in1=q1, op=ALU.add), reads=[q0b, q1b], writes=[qsb])
            k = 4.0 if half_scale else 1.0
            sch.add("pool", I("tensor_scalar", out=rms, in0=qs, scalar1=k / D, scalar2=k * EPS, op0=ALU.mult, op1=ALU.add),
                    reads=[qsb], writes=[rmsb])
            sch.add("pool", I("tensor_tensor", out=rstd, in0=rms, in1=cst[:, 546:547], op=ALU.pow), reads=[rmsb, buf("cst")], writes=[rstdb])
            for dh, (pb, bb) in enumerate(((pb0, bb0), (pb1, bb1))):
                sch.add("dve", I("scalar_tensor_tensor",
                    out=ytile[:, dh * 512:(dh + 1) * 512], in0=pb[:], scalar=rstd, in1=gbc[:, gpar, dh * 512:(dh + 1) * 512],
                    op0=ALU.mult, op1=ALU.mult), reads=[bb, rstdb, buf("gbc%d" % gpar)], writes=[ytb])
            sch.add("pool", I("tensor_tensor", out=xld[:, si, :], in0=ytile, in1=xld[:, si, :], op=ALU.add),
                    reads=[ytb, sbuf_], writes=[sbuf_])
            sch.add("pool", I("dma_start", out=dst[row0:row0 + 128, :], in_=xld[:, si, :]),
                    reads=[sbuf_], writes=[buf("xd%d" % (row0 // 128))], dma="st%d" % si)

        jit = {"pending": [], "loaded": None, "cast": None, "k": 0}

        def jit_views(u):
            typ, W, u0, n, cid, k = u
            sf = jstg_f[k % 2]
            sbf = jstg_b[k % 2]
            if typ == "col":
                src_ = W[:, u0 * 128:(u0 + n) * 128].rearrange("(kc p) c -> p kc c", p=128)
                dstv = sf[:, 0:8 * n * 128].rearrange("p (kc c) -> p kc c", kc=8)
                cin = sf[:, 0:8 * n * 128].rearrange("p (kc j c) -> p j kc c", kc=8, j=n)
                cout = sbf[:, 0:n * 1024].rearrange("p (j kc c) -> p j kc c", j=n, kc=8)
            else:
                src_ = W[u0 * 128:(u0 + n) * 128, :].rearrange("(f p) c -> p f c", p=128)
                dstv = sf[:, 0:n * 1024].rearrange("p (f c) -> p f c", f=n)
                cin = sf[:, 0:n * 1024]
                cout = sbf[:, 0:n * 1024]
            return src_, dstv, cin, cout, sbf

        def jit_step():
            if jit["cast"] is not None:
                u = jit["cast"]
                typ, W, u0, n, cid, k = u
                src_, dstv, cin, cout, sbf = jit_views(u)
                sch.add("sp", I("dma_start", out=wsc[:, cid:cid + n, :], in_=sbf[:, 0:n * 1024].rearrange("p (j c) -> p j c", j=n)),
                        reads=[buf("jstgb%d" % (k % 2))], writes=[wscb[cid]], dma="js%d" % (k % 2))
                jit["cast"] = None
            if jit["loaded"] is not None:
                u = jit["loaded"]
                typ, W, u0, n, cid, k = u
                src_, dstv, cin, cout, sbf = jit_views(u)
                if k % 2 == 0:
                    sch.add("dve", I("tensor_copy", out=cout, in_=cin), reads=[buf("jstgf%d" % (k % 2))], writes=[buf("jstgb%d" % (k % 2))])
                else:
                    sch.add("act", I("activation", out=cout, in_=cin, func=AF.Copy), reads=[buf("jstgf%d" % (k % 2))], writes=[buf("jstgb%d" % (k % 2))])
                jit["cast"] = u
                jit["loaded"] = None
            if jit["pending"]:
                u = jit["pending"].pop(0)
                typ, W, u0, n, cid, k = u
                src_, dstv, cin, cout, sbf = jit_views(u)
                sch.add("sp", I("dma_start", out=dstv, in_=src_), writes=[buf("jstgf%d" % (k % 2))], dma="jl%d" % (k % 2))
                jit["loaded"] = u

        def jit_busy():
            return bool(jit["pending"]) or jit["loaded"] is not None or jit["cast"] is not None

        def jit_enqueue(pi):
            for pj in range(pi + 1, len(phases)):
                for (typ, W, u0, n, cid) in phase_units(*phases[pj], 2):
                    wb = buf("wsc%d" % cid)
                    for k_ in range(n):
                        wscb[cid + k_] = wb
                    jit["pending"].append((typ, W, u0, n, cid, jit["k"]))
                    jit["k"] += 1
                if phases[pj][0] == "ffn":
                    break

        pre_done = [False]
        for pi, (kind, l, j) in enumerate(phases):
            src = x_in if pi == 0 else xbuf
            dst = y_out if pi == len(phases) - 1 else xbuf
            gpar = pi % 2
            jpre, jpost = (4 * j, 4 * j + 1) if kind == "ffn" else (2, 3)
            sch.add("sp", I("dma_start",
                out=gbc[:, gpar, :], in_=norm_g[l, jpost:jpost + 1, :].partition_broadcast(128)),
                writes=[buf("gbc%d" % gpar)], dma="gb%d" % gpar)
            if kind == "ffn":
                jit_enqueue(pi)
                cd = chunk_id("d", l, j, 0)
                sch.add("sp", I("dma_start", out=Wd[:, 0:11, :], in_=wsc[:, cd:cd + 11, :]),
                        reads=[wscb[cd + k] for k in range(11)], writes=[buf("Wd")], dma="wd0")
                sch.add("sp", I("dma_start", out=Wd[:, 11:22, :], in_=wsc[:, cd + 11:cd + 22, :]),
                        reads=[wscb[cd + k] for k in range(11, 22)], writes=[buf("Wd")], dma="wd1")
                for G in range(4):
                    if G == 0 and not pre_done[0]:
                        for tt in range(8):
                            prenorm_tile(src, tt * 128, l, jpre, xnT_f, tt * 128, tt % 4)
                    pre_done[0] = False
                    nxt = None
                    if G < 3:
                        nxt = (src, (G + 1) * 1024, l, jpre)
                    elif pi + 1 < len(phases) and phases[pi + 1][0] == "ffn":
                        nxt = (xbuf, 0, phases[pi + 1][1], 4 * phases[pi + 1][2])
                        pre_done[0] = True
                    for f in range(NF):
                        jit_step()
                        wg, wgb = ring_acquire(("g", l, j, f))
                        wgv = wg.rearrange("p (kc c) -> p kc c", kc=8)
                        for half in range(2):
                            for kc in range(8):
                                sch.add("pe", I("matmul",
                                    banks[half][:], lhsT=wgv[:, kc, :], rhs=xnT_f[:, kc, half * 512:(half + 1) * 512],
                                    start=(kc == 0), stop=(kc == 7)), reads=[wgb, buf("xnT")], writes=[bankb[half]])
                        ring_release()
                        wu, wub = ring_acquire(("u", l, j, f))
                        wuv = wu.rearrange("p (kc c) -> p kc c", kc=8)
                        for half in range(2):
                            for kc in range(8):
                                sch.add("pe", I("matmul",
                                    banks[2 + half][:], lhsT=wuv[:, kc, :], rhs=xnT_f[:, kc, half * 512:(half + 1) * 512],
                                    start=(kc == 0), stop=(kc == 7)), reads=[wub, buf("xnT")], writes=[bankb[2 + half]])
                        ring_release()
                        for half in range(2):
                            sch.add("act", I("activation", out=sg[:, half, :], in_=banks[half][:], func=AF.Silu),
                                    reads=[bankb[half]], writes=[buf("sg%d" % half)])
                            sch.add("dve", I("tensor_tensor",
                                out=hT[:, f, half * 512:(half + 1) * 512], in0=sg[:, half, :], in1=banks[2 + half][:], op=ALU.mult),
                                reads=[buf("sg%d" % half), bankb[2 + half]], writes=[buf("hT%d" % f)])
                    for tt in range(8):
                        b0 = 4 + 2 * (tt % 2)
                        pst = None
                        if nxt is not None:
                            pst = prenorm_a(nxt[0], nxt[1] + tt * 128, nxt[2], nxt[3], xnT_f, tt * 128, tt % 4)
                        for dh in range(2):
                            for f in range(NF):
                                sch.add("pe", I("matmul",
                                    banks[b0 + dh][:], lhsT=hT[:, f, tt * 128:(tt + 1) * 128], rhs=Wd[:, f, dh * 512:(dh + 1) * 512],
                                    start=(f == 0), stop=(f == NF - 1)), reads=[buf("hT%d" % f), buf("Wd")], writes=[bankb[b0 + dh]])
                        if pst is not None:
                            prenorm_b(pst)
                        postnorm_tile(banks[b0], banks[b0 + 1], bankb[b0], bankb[b0 + 1], src, dst, G * 1024 + tt * 128,
                                      gpar, True, ytmp[:, tt % 2, :], buf("ytmp%d" % (tt % 2)), False)
                while jit_busy():
                    jit_step()
            else:
                cwcol = lambda tap, ch: paramT[:, 192 + (l * 3 + tap) * 2 + ch:192 + (l * 3 + tap) * 2 + ch + 1]
                pscol = lambda ch: paramT[:, 216 + l * 2 + ch:216 + l * 2 + ch + 1]
                sch.add("pool", I("memset", ubuf[:, :, 0:2], 0.0), writes=[buf("ubuf")])
                sch.add("pool", I("memset", pbuf[:, :, 0:16], 0.0), writes=[buf("pbuf")])
                sch.add("pool", I("memset", qT[:], 0.0), writes=[buf("qT%d" % c_) for c_ in range(4)])
                bgb = [0]

                def nextbank():
                    bgb[0] += 1
                    return 6 + bgb[0] % 2

                def proj_fm(ch, bank):
                    w, wb = ring_acquire(("wi", l, 0, ch))
                    wv = w.rearrange("p (kc c) -> p kc c", kc=8)
                    for kc in range(8):
                        sch.add("pe", I("matmul", banks[bank][:], lhsT=wv[:, kc, :], rhs=xnT_m[:, kc, :], start=(kc == 0), stop=(kc == 7)),
                                reads=[wb, buf("xnT")], writes=[bankb[bank]])
                    ring_release()

                def t_pre(i):
                    def mk(tt):
                        return lambda: prenorm_tile(src, i * 512 + tt * 128, l, jpre, xnT_m, tt * 128, 6 + tt % 2)
                    return [mk(tt) for tt in range(4)]

                def t_q(i, c):
                    def f():
                        bk = nextbank()
                        proj_fm(c, bk)
                        for hh in range(2):
                            hp_ = slice(hh * 64, hh * 64 + 64)
                            sch.add("dve", I("tensor_scalar", out=qT[hp_, c, hh, :], in0=banks[bk][hp_, :], scalar1=0.125, scalar2=None, op0=ALU.mult),
                                    reads=[bankb[bk]], writes=[buf("qT%d" % c)])
                    return [f]

                def t_k(i, c):
                    def f():
                        bk = nextbank()
                        proj_fm(4 + c, bk)
                        sch.add("dve", I("tensor_copy", out=kT[:, c, i * 512:(i + 1) * 512], in_=banks[bk][:]),
                                reads=[bankb[bk]], writes=[buf("kT%d_%d" % (c, i))])
                    return [f]

                def t_v(i, jv):
                    def f():
                        bk = nextbank()
                        w, wb = ring_acquire(("wi", l, 0, 8 + jv))
                        wv = w.rearrange("p (kc c) -> p kc c", kc=8)
                        for tt in range(4):
                            for kc in range(8):
                                sch.add("pe", I("matmul", banks[bk][:, tt * 128:(tt + 1) * 128], lhsT=xnT_m[:, kc, tt * 128:(tt + 1) * 128], rhs=wv[:, kc, :],
                                                start=(kc == 0), stop=(kc == 7)), reads=[wb, buf("xnT")], writes=[bankb[bk]])
                        ring_release()
                        sch.add("dve", I("tensor_copy", out=vS[:, 4 * i:4 * i + 4, jv * 128:(jv + 1) * 128], in_=banks[bk][:].rearrange("p (t c) -> p t c", t=4)),
                                reads=[bankb[bk]], writes=[buf("v%d_%d" % (i, jv))])
                    return [f]

                def t_convpool(i):
                    sl = []

                    def mk_bc(ch, which):
                        def f():
                            bk = nextbank()
                            proj_fm((12 if which == "B" else 14) + ch, bk)
                            dst_ = Bsb if which == "B" else Csb
                            sch.add("dve", I("tensor_copy", out=dst_[:, ch, :], in_=banks[bk][:]),
                                    reads=[bankb[bk]], writes=[buf("%ssb%d" % (which, ch))])
                        return f

                    def mk_h(ch):
                        def f():
                            bk = nextbank()
                            proj_fm(16 + ch, bk)
                            ub = buf("ubuf%d" % ch)
                            yb = buf("ycv%d" % ch)
                            sch.add("dve", I("tensor_tensor", out=ubuf[:, ch, 2:514], in0=banks[bk][:], in1=Csb[:, ch, :], op=ALU.mult),
                                    reads=[bankb[bk], buf("Csb%d" % ch), buf("ubuf")], writes=[ub])
                            sch.add("dve", I("tensor_scalar", out=ycv[:, ch, :], in0=ubuf[:, ch, 2:514], scalar1=cwcol(2, ch), scalar2=None, op0=ALU.mult),
                                    reads=[ub, buf("paramT")], writes=[yb])
                            sch.add("dve", I("scalar_tensor_tensor", out=ycv[:, ch, :], in0=ubuf[:, ch, 1:513], scalar=cwcol(1, ch), in1=ycv[:, ch, :],
                                             op0=ALU.mult, op1=ALU.add), reads=[ub, buf("paramT"), yb], writes=[yb])
                            sch.add("dve", I("scalar_tensor_tensor", out=ycv[:, ch, :], in0=ubuf[:, ch, 0:512], scalar=cwcol(0, ch), in1=ycv[:, ch, :],
                                             op0=ALU.mult, op1=ALU.add), reads=[ub, buf("paramT"), yb], writes=[yb])
                            sch.add("dve", I("tensor_tensor", out=mixT[:, 4 + ch, :], in0=ycv[:, ch, :], in1=Bsb[:, ch, :], op=ALU.mult),
                                    reads=[yb, buf("Bsb%d" % ch)], writes=[buf("mixT_%d" % (4 + ch))])
                            sch.add("dve", I("tensor_copy", out=ubuf[:, ch, 0:2], in_=ubuf[:, ch, 512:514]), reads=[ub], writes=[ub])
                        return f

                    def mk_p(ch):
                        def f():
                            bk = nextbank()
                            proj_fm(18 + ch, bk)
                            sch.add("dve", I("tensor_copy", out=pbuf[:, ch, 16:528], in_=banks[bk][:]),
                                    reads=[bankb[bk]], writes=[buf("pbuf")])
                        return f

                    def f_pool():
                        pb_ = buf("pbuf")
                        sb_ = buf("pools")
                        sch.add("pool", I("tensor_tensor", out=s2[:, :, 1:528], in0=pbuf[:, :, 1:528], in1=pbuf[:, :, 0:527], op=ALU.add),
                                reads=[pb_], writes=[sb_])
                        sch.add("pool", I("tensor_tensor", out=s4[:, :, 3:528], in0=s2[:, :, 3:528], in1=s2[:, :, 1:526], op=ALU.add),
                                reads=[sb_], writes=[buf("pools4")])
                        sch.add("pool", I("tensor_tensor", out=s8[:, 7:528], in0=s4[:, 1, 7:528], in1=s4[:, 1, 3:524], op=ALU.add),
                                reads=[buf("pools4")], writes=[buf("pools8")])
                        sch.add("pool", I("tensor_tensor", out=s16[:, 15:528], in0=s8[:, 15:528], in1=s8[:, 7:520], op=ALU.add),
                                reads=[buf("pools8")], writes=[buf("pools16")])
                        dTb = buf("dT")
                        sels = ((0, 0, s2[0:64, 0, :], sb_), (0, 1, s4[64:128, 0, :], buf("pools4")),
                                (1, 0, s8[0:64, :], buf("pools8")), (1, 1, s16[64:128, :], buf("pools16")))
                        for ch, gp, sv, svb in sels:
                            pr = slice(gp * 64, gp * 64 + 64)
                            sch.add("dve", I("scalar_tensor_tensor", out=dT[pr, ch, :], in0=sv[:, 16:528], scalar=cst[pr, 544 + ch:545 + ch], in1=pbuf[pr, ch, 16:528],
                                             op0=ALU.mult, op1=ALU.subtract), reads=[svb, pb_, buf("cst")], writes=[dTb])
                            if i == 0:
                                sch.add("dve", I("tensor_tensor", out=t16[pr, ch, :], in0=sv[:, 16:32], in1=cst[pr, 512 + ch * 16:528 + ch * 16], op=ALU.mult),
                                        reads=[svb, buf("cst")], writes=[buf("t16")])
                                sch.add("dve", I("tensor_tensor", out=dT[pr, ch, 0:16], in0=t16[pr, ch, :], in1=pbuf[pr, ch, 16:32], op=ALU.subtract),
                                        reads=[buf("t16"), pb_], writes=[dTb])
                        for ch in range(2):
                            bk = nextbank()
                            sch.add("pe", I("matmul", banks[bk][:], lhsT=BD[:, l * 2 + ch, :], rhs=dT[:, ch, :], start=True, stop=True),
                                    reads=[buf("BD"), dTb], writes=[bankb[bk]])
                            sch.add("dve", I("tensor_scalar", out=mixT[:, 6 + ch, :], in0=banks[bk][:], scalar1=pscol(ch), scalar2=None, op0=ALU.mult),
                                    reads=[bankb[bk], buf("paramT")], writes=[buf("mixT_%d" % (6 + ch))])
                        sch.add("pool", I("tensor_copy", out=pbuf[:, :, 0:16], in_=pbuf[:, :, 512:528]), reads=[pb_], writes=[pb_])

                    for ch in range(2):
                        sl.append(mk_bc(ch, "B"))
                    for ch in range(2):
                        sl.append(mk_bc(ch, "C"))
                    for ch in range(2):
                        sl.append(mk_h(ch))
                    for ch in range(2):
                        sl.append(mk_p(ch))
                    sl.append(f_pool)
                    return sl

                def t_out(i):
                    sl = []

                    def mk(tt, part):
                        def f():
                            for kc in range(4 * part, 4 * part + 4):
                                w, wb = ring_acquire(("wo", l, 0, kc))
                                if kc < 4:
                                    rds = [buf("mixT_%d_0" % kc), buf("mixT_%d_1" % kc)]
                                else:
                                    rds = [buf("mixT_%d" % kc)]
                                for dh in range(2):
                                    sch.add("pe", I("matmul", banks[6 + dh][:], lhsT=mixT[:, kc, tt * 128:(tt + 1) * 128], rhs=w[:, dh * 512:(dh + 1) * 512],
                                                    start=(kc == 0), stop=(kc == 7)), reads=[wb] + rds, writes=[bankb[6 + dh]])
                                ring_release()
                            if part == 1:
                                postnorm_tile(banks[6], banks[7], bankb[6], bankb[7], src, dst, i * 512 + tt * 128,
                                              gpar, False, ytmp_m[:, tt % 2, :], buf("ytmpm%d" % (tt % 2)), False)
                        return f
                    for tt in range(4):
                        sl.append(mk(tt, 0))
                        sl.append(mk(tt, 1))
                    return sl

                def attn_stages(i, steps):
                    kb_last = 4 * i + 3

                    def info(n):
                        c, kb, hh = steps[n]
                        r = kb - 4 * i
                        c0 = max(r, 0) * 128
                        return c, kb, hh, r, c0, slice(hh * 64, hh * 64 + 64)

                    def st_z(n):
                        c, kb, hh, r, c0, hp = info(n)
                        zb = n % 2
                        sch.add("pe", I("matmul", banks[zb][:, c0:512], lhsT=kT[:, c, kb * 128:(kb + 1) * 128], rhs=qT[:, c, hh, c0:512],
                                        start=True, stop=(r < 0)),
                                reads=[buf("kT%d_%d" % (c, kb // 4)), buf("qT%d" % c)], writes=[bankb[zb]])
                        if r >= 0:
                            sch.add("pe", I("matmul", banks[zb][:, c0:c0 + 128], lhsT=ident_b[:], rhs=nmask_b[:], start=False, stop=True),
                                    reads=[buf("ident_b"), buf("nmask_b")], writes=[bankb[zb]])

                    def st_e(n):
                        c, kb, hh, r, c0, hp = info(n)
                        zb = n % 2
                        sch.add("act", I("activation", out=banks[zb][:, c0:512], in_=banks[zb][:, c0:512], func=AF.Exp),
                                reads=[bankb[zb]], writes=[bankb[zb]])

                    def st_sp(n):
                        c, kb, hh, r, c0, hp = info(n)
                        zb = n % 2
                        k3 = n % 3
                        sch.add("act", I("activation", out=SPt[:, k3, c0:512], in_=banks[zb][:, c0:512], func=AF.Ln, bias=1.0),
                                reads=[bankb[zb]], writes=[buf("SP%d" % k3)])

                    def st_b(n):
                        c, kb, hh, r, c0, hp = info(n)
                        bbk = 2 + n % 2
                        k3 = n % 3
                        first = (kb == kb_last)
                        rp = (kb_last - kb) % 2
                        Rold, Roldb = Rt[:, hh * 2 + (1 - rp), :], buf("R%d" % (hh * 2 + 1 - rp))
                        Rnew, Rnewb = Rt[:, hh * 2 + rp, :], buf("R%d" % (hh * 2 + rp))
                        sch.add("pe", I("matmul", banks[bbk][:, c0:512], lhsT=kT[:, c, kb * 128:(kb + 1) * 128], rhs=qT[:, c, hh, c0:512],
                                        start=True, stop=False),
                                reads=[buf("kT%d_%d" % (c, kb // 4)), buf("qT%d" % c)], writes=[bankb[bbk]])
                        if r >= 0:
                            sch.add("pe", I("matmul", banks[bbk][:, c0:c0 + 128], lhsT=ident_b[:], rhs=nmask_b[:], start=False, stop=False),
                                    reads=[buf("ident_b"), buf("nmask_b")], writes=[bankb[bbk]])
                        cc = c0 + 128 if r >= 0 else 0
                        if not first and cc < 512:
                            sch.add("pe", I("matmul", banks[bbk][:, cc:512], lhsT=nones_b[:], rhs=Rold[:, cc:512], start=False, stop=False),
                                    reads=[buf("nones_b"), Roldb], writes=[bankb[bbk]])
                        sch.add("pe", I("matmul", banks[bbk][:, c0:512], lhsT=uneg_b[:], rhs=SPt[:, k3, c0:512], start=False, stop=True),
                                reads=[buf("uneg_b"), buf("SP%d" % k3)], writes=[bankb[bbk]])
                        if kb > 0:
                            if r >= 0:
                                sch.add("dve", I("tensor_copy", out=Rnew[:, c0:c0 + 128], in_=SPt[:, k3, c0:c0 + 128]),
                                        reads=[buf("SP%d" % k3)], writes=[Rnewb])
                            if cc < 512 and not first:
                                sch.add("dve", I("tensor_tensor", out=Rnew[:, cc:512], in0=Rold[:, cc:512], in1=SPt[:, k3, cc:512], op=ALU.add),
                                        reads=[buf("SP%d" % k3), Roldb], writes=[Rnewb])

                    def st_a(n):
                        c, kb, hh, r, c0, hp = info(n)
                        bbk = 2 + n % 2
                        k3 = n % 3
                        sch.add("act", I("activation", out=At[:, k3, c0:512], in_=banks[bbk][:, c0:512], func=AF.Exp),
                                reads=[bankb[bbk]], writes=[buf("A%d" % k3)])

                    def st_av(n):
                        c, kb, hh, r, c0, hp = info(n)
                        k3 = n % 3
                        ob = 4 + hh
                        sch.add("pe", I("matmul", banks[ob][:, c0:512], lhsT=vS[:, kb, c * 128:(c + 1) * 128], rhs=At[:, k3, c0:512],
                                        start=(kb == kb_last), stop=(kb == 0), skip_group_check=(kb != kb_last and r >= 0)),
                                reads=[buf("v%d_%d" % (kb // 4, c)), buf("A%d" % k3)], writes=[bankb[ob]])
                        if kb == 0:
                            sch.add("dve", I("tensor_copy", out=mixT[hp, c, :], in_=banks[ob][hp, :]),
                                    reads=[bankb[ob]], writes=[buf("mixT_%d_%d" % (c, hh))])
                    return st_z, st_e, st_sp, st_b, st_a, st_av

                for s_ in t_pre(0):
                    s_()
                for c in range(4):
                    for s_ in t_q(0, c) + t_k(0, c) + t_v(0, c):
                        s_()
                for s_ in t_convpool(0):
                    s_()

                for i in range(8):
                    kb_last = 4 * i + 3
                    steps = []
                    for c in range(4):
                        for kb in range(kb_last, -1, -1):
                            for hh in range(2):
                                steps.append((c, kb, hh))
                    NS = len(steps)
                    P = NS // 4
                    st_z, st_e, st_sp, st_b, st_a, st_av = attn_stages(i, steps)
                    tasks = []
                    if i > 0:
                        tasks += [(0, "q3", s_) for s_ in t_q(i, 3)]
                        tasks += [(0, "out", s_) for s_ in t_out(i - 1)]
                        tasks += [(0, "cp", s_) for s_ in t_convpool(i)]
                    if i < 7:
                        tasks += [(0, "pre", s_) for s_ in t_pre(i + 1)]
                        for c in range(4):
                            tasks += [(0, "kv", s_) for s_ in t_k(i + 1, c) + t_v(i + 1, c)]
                        for c in range(3):
                            tasks += [((c + 1) * P + 3, "q", s_) for s_ in t_q(i + 1, c)]
                    ntask = len(tasks)
                    pulled = [0]

                    def flush_tag(tag):
                        last = -1
                        for ti, (rd, tg, fn_) in enumerate(tasks):
                            if tg == tag:
                                last = ti
                        while pulled[0] <= last:
                            tasks[pulled[0]][2]()
                            pulled[0] += 1

                    for it in range(NS + 4):
                        if i > 0 and it == P:
                            flush_tag("out")
                        if i > 0 and it == 3 * P:
                            flush_tag("q3")
                        if it < NS:
                            st_z(it)
                        if 0 <= it - 1 < NS:
                            st_e(it - 1)
                        if 0 <= it - 3 < NS:
                            st_a(it - 3)
                        if 0 <= it - 1 < NS:
                            st_sp(it - 1)
                        if 0 <= it - 2 < NS:
                            st_b(it - 2)
                        if 0 <= it - 4 < NS:
                            st_av(it - 4)
                        target = (it + 1) * ntask // NS
                        while pulled[0] < min(target, ntask) and tasks[pulled[0]][0] <= it:
                            tasks[pulled[0]][2]()
                            pulled[0] += 1
                    while pulled[0] < ntask:
                        tasks[pulled[0]][2]()
                        pulled[0] += 1
                for s_ in t_out(7):
                    s_()
            if not (kind == "ffn" and pi + 1 < len(phases) and phases[pi + 1][0] == "ffn"):
                sch.barrier()

        if order_in is None:
            return rec_order
        sch.emit(nc, st)
    return nc


_CACHE = {}


def run(inputs, nphases=12):
    if nphases not in _CACHE:
        _CACHE[nphases] = build_program(nphases)
    nc = _CACHE[nphases]
    cst = make_consts()
    shared = {k: np.ascontiguousarray(np.asarray(inputs[k], dtype=np.float32)) for k in
              ("norm_g", "ffn_w_gate", "ffn_w_up", "ffn_w_down", "w_in", "conv_w", "pool_w", "pool_scale", "w_out")}
    x = np.asarray(inputs["x"], dtype=np.float32)
    in_maps = []
    for b in range(NCORES):
        m = dict(shared)
        m["x"] = np.ascontiguousarray(x[b])
        m["consts"] = cst
        in_maps.append(m)
    res = run_bass_kernel_spmd(nc, in_maps, core_ids=list(range(NCORES)))
    return np.stack([np.asarray(r["y"], dtype=np.float32) for r in res.results], axis=0)


def kernel(**inputs):
    return run(inputs, 12)
```

```python
import contextlib
import numpy as np
import concourse.bass as bass
import concourse.mybir as mybir
from concourse.bass_utils import run_bass_kernel_spmd

F32 = mybir.dt.float32
BF16 = mybir.dt.bfloat16
AF = mybir.ActivationFunctionType
ALU = mybir.AluOpType

NCORES = 8
S = 4096
D = 1024
DFF = 2816
NF = 22
DEPTH = 4
NT = S // 128
EPS = 1e-6
NSLOT = 8
CH_PER_LAYER = 160
POOL_W = (2, 4, 8, 16)


class Buf:
    __slots__ = ("name", "w", "r")

    def __init__(self, name):
        self.name = name
        self.w = None
        self.r = []


class Op:
    __slots__ = ("eng", "fn", "deps", "is_dma", "key", "sig", "has_dep")

    def __init__(self, eng, fn, is_dma, key):
        self.eng = eng
        self.fn = fn
        self.deps = []
        self.is_dma = is_dma
        self.key = key
        self.sig = None
        self.has_dep = False


class Sched:
    ENGS = ("pe", "act", "dve", "pool", "sp")

    def __init__(self):
        self.q = {e: [] for e in self.ENGS}
        self.dma_keys = []
        self.last_dma = {}
        self.pending_barrier = {e: None for e in self.ENGS}

    def add(self, eng, fn, reads=(), writes=(), dma=None):
        op = Op(eng, fn, dma is not None, dma)
        if dma is not None:
            if dma not in self.dma_keys:
                self.dma_keys.append(dma)
            self.last_dma[dma] = op
        deps = []
        for b in reads:
            if b.w is not None:
                deps.append((b.w, "raw"))
        for b in writes:
            if b.w is not None:
                deps.append((b.w, "waw"))
            for r in b.r:
                deps.append((r, "war"))
        seen = set()
        for d, kind in deps:
            if d is op or id(d) in seen:
                continue
            if (not d.is_dma) and (not op.is_dma) and d.eng == eng:
                if eng == "pe" or kind != "raw":
                    continue
            seen.add(id(d))
            op.deps.append(d)
            d.has_dep = True
        pb = self.pending_barrier[eng]
        if pb is not None:
            for d in pb:
                if id(d) not in seen and d is not op:
                    seen.add(id(d))
                    op.deps.append(d)
                    d.has_dep = True
            self.pending_barrier[eng] = None
        for b in writes:
            b.w = op
            b.r = []
        for b in reads:
            if b.w is not op:
                b.r.append(op)
        self.q[eng].append(op)
        return op

    def barrier(self):
        ops = []
        for e in self.ENGS:
            for op in reversed(self.q[e]):
                if not op.is_dma:
                    ops.append(op)
                    break
        ops.extend(self.last_dma.values())
        for e in self.ENGS:
            prev = self.pending_barrier[e]
            self.pending_barrier[e] = list(ops) if prev is None else prev + ops

    def emit(self, nc, st):
        esem = {e: st.enter_context(nc.semaphore("s_" + e)) for e in ("pe", "act", "dve", "pool", "sp")}
        dsem = {k: st.enter_context(nc.semaphore("d_" + k)) for k in self.dma_keys}
        finals = list(self.last_dma.values())
        cnt = {e: 0 for e in esem}
        dcnt = {k: 0 for k in dsem}
        for e in self.ENGS:
            for op in self.q[e]:
                if op.is_dma:
                    dcnt[op.key] += 16
                    op.sig = (dsem[op.key], dcnt[op.key])
                elif op.has_dep:
                    cnt[e] += 1
                    op.sig = (esem[e], cnt[e])
        block = st.enter_context(nc.Block())
        engmap = {"pe": "tensor", "act": "scalar", "dve": "vector", "pool": "gpsimd", "sp": "sync"}

        def make(e):
            def body(engine):
                known = {}
                for op in self.q[e]:
                    for d in op.deps:
                        sem, val = d.sig
                        kk = id(sem)
                        if known.get(kk, 0) >= val:
                            continue
                        engine.wait_ge(sem, val)
                        known[kk] = val
                    name_, args_, kw_ = op.fn
                    ins = getattr(engine, name_)(*args_, **kw_)
                    if op.sig is not None:
                        ins.then_inc(op.sig[0], 16 if op.is_dma else 1)
                if e == "sp":
                    for op in finals:
                        sem, val = op.sig
                        if known.get(id(sem), 0) < val:
                            engine.wait_ge(sem, val)
            return body

        for e in self.ENGS:
            getattr(block, engmap[e])(make(e))


def I(name, *args, **kwargs):
    return (name, args, kwargs)


def make_consts():
    c = np.zeros((128, 548), np.float32)
    p = np.arange(128)[:, None]
    f = np.arange(128)[None, :]
    c[:, 0:128] = np.eye(128, dtype=np.float32)
    c[:, 128:256] = np.where(p >= f, -1.0, 0.0)
    c[:, 256:384] = -1.0
    c[:, 384:512] = np.where(p >= f, -30000.0, 0.0)
    for ch in range(2):
        for pp in range(128):
            w = POOL_W[2 * ch + pp // 64]
            for t in range(16):
                c[pp, 512 + ch * 16 + t] = 1.0 / min(t + 1, w)
            c[pp, 544 + ch] = 1.0 / w
    c[:, 546] = -0.5
    return c


def chunk_id(kind, l, j, idx):
    base = l * CH_PER_LAYER
    if kind == "g":
        return base + j * 66 + idx
    if kind == "u":
        return base + j * 66 + 22 + idx
    if kind == "d":
        return base + j * 66 + 44 + idx
    if kind == "wi":
        return base + 132 + idx
    if kind == "wo":
        return base + 152 + idx
    raise ValueError(kind)


def phase_list(nphases):
    ph = []
    for l in range(DEPTH):
        ph.append(("ffn", l, 0))
        ph.append(("mix", l, 0))
        ph.append(("ffn", l, 1))
    return ph[:nphases]


def ring_order(phases):
    order = []
    for kind, l, j in phases:
        if kind == "ffn":
            for G in range(4):
                for f in range(NF):
                    order.append(("g", l, j, f))
                    order.append(("u", l, j, f))
        else:
            for i in range(8):
                for ch in range(20):
                    order.append(("wi", l, 0, ch))
                for tt in range(4):
                    for kc in range(8):
                        order.append(("wo", l, 0, kc))
    return order


def build_program(nphases=12):
    rec = _build(nphases, None)
    return _build(nphases, rec)


def _build(nphases, order_in):
    nc = bass.Bass("TRN2", target_bir_lowering=False)
    phases = phase_list(nphases)
    layers_needed = sorted({l for _, l, _ in phases})

    def din(name, shape):
        return nc.dram_tensor(name, shape, F32, kind="ExternalInput").ap()

    x_in = din("x", [S, D])
    norm_g = din("norm_g", [DEPTH, 6, D])
    w_gate = din("ffn_w_gate", [DEPTH, 2, D, DFF])
    w_up = din("ffn_w_up", [DEPTH, 2, D, DFF])
    w_down = din("ffn_w_down", [DEPTH, 2, DFF, D])
    w_in = din("w_in", [DEPTH, D, 2560])
    conv_w = din("conv_w", [DEPTH, 3, 256])
    pool_w = din("pool_w", [DEPTH, 4, 64, 64])
    pool_scale = din("pool_scale", [DEPTH, 256])
    w_out = din("w_out", [DEPTH, D, D])
    consts = din("consts", [128, 548])
    y_out = nc.dram_tensor("y", [S, D], F32, kind="ExternalOutput").ap()
    xbuf = nc.dram_tensor("xbuf", [S, D], F32, kind="Internal").ap()
    wsc = nc.dram_tensor("wsc", [128, DEPTH * CH_PER_LAYER, 1024], BF16, kind="Internal").ap()

    sch = Sched()
    st = contextlib.ExitStack()
    with st:
        def sb(name, shape, dt):
            return st.enter_context(nc.sbuf_tensor(name, shape, dt))

        cst = sb("cst", [128, 548], F32)
        ident_b = sb("ident_b", [128, 128], BF16)
        uneg_b = sb("uneg_b", [128, 128], BF16)
        nones_b = sb("nones_b", [128, 128], BF16)
        nmask_b = sb("nmask_b", [128, 128], BF16)
        paramT = sb("paramT", [128, 224], F32)
        BD = sb("BD", [128, 8, 128], BF16)
        ring = sb("ring", [128, NSLOT, 1024], BF16)
        xld = sb("xld", [128, 4, 1024], F32)
        gbc = sb("gbc", [128, 2, 1024], F32)
        junk = sb("junk", [128, 1024], BF16)
        stats = sb("stats", [128, 64], F32)
        xs = sb("xs", [128, 2, 1024], BF16)
        OVL = 148 * 1024 // 2
        big = sb("big", [128, OVL], BF16)
        banks = [st.enter_context(nc.psum_tensor("bank%d" % i, [128, 512], F32)) for i in range(8)]
        bankb = [Buf("bank%d" % i) for i in range(8)]

        ident_f = cst[:, 0:128]

        class Carver:
            def __init__(self):
                self.off = 0

            def take(self, shape, dt):
                n = 1
                for s_ in shape[1:]:
                    n *= s_
                nb = n * (4 if dt == F32 else 2)
                nb = (nb + 63) // 64 * 64
                o = self.off
                self.off += nb
                assert self.off <= OVL * 2, ("overlay overflow", self.off)
                v = big[:, o // 2:(o + nb) // 2]
                if dt == F32:
                    v = v.bitcast(F32)
                v = v[:, 0:n]
                if len(shape) == 3:
                    v = v.rearrange("p (a b) -> p a b", a=shape[1])
                elif len(shape) == 4:
                    v = v.rearrange("p (a b c) -> p a b c", a=shape[1], b=shape[2])
                return v

        cv = Carver()
        stg_f = [cv.take([128, 4096], F32) for _ in range(2)]
        stg_b = [cv.take([128, 4096], BF16) for _ in range(2)]
        pstage = cv.take([128, 2, 128], F32)
        bdstage = cv.take([128, 8, 128], F32)
        cv = Carver()
        hT = cv.take([128, NF, 1024], BF16)
        xnT_f = cv.take([128, 8, 1024], BF16)
        Wd = cv.take([128, NF, 1024], BF16)
        ytmp = cv.take([128, 2, 1024], F32)
        sg = cv.take([128, 2, 512], F32)
        jstg_f = [cv.take([128, 2048], F32) for _ in range(2)]
        jstg_b = [cv.take([128, 2048], BF16) for _ in range(2)]
        cv = Carver()
        kT = cv.take([128, 4, S], BF16)
        vS = cv.take([128, NT, 512], BF16)
        xnT_m = cv.take([128, 8, 512], BF16)
        qT = cv.take([128, 4, 2, 512], BF16)
        mixT = cv.take([128, 8, 512], BF16)
        SPt = cv.take([128, 3, 512], BF16)
        At = cv.take([128, 3, 512], BF16)
        Rt = cv.take([128, 4, 512], BF16)
        Bsb = cv.take([128, 2, 512], F32)
        Csb = cv.take([128, 2, 512], F32)
        ubuf = cv.take([128, 2, 514], F32)
        ycv = cv.take([128, 2, 512], F32)
        pbuf = cv.take([128, 2, 528], F32)
        s2 = cv.take([128, 2, 528], F32)
        s4 = cv.take([128, 2, 528], F32)
        s8 = cv.take([128, 528], F32)
        s16 = cv.take([128, 528], F32)
        dT = cv.take([128, 2, 512], BF16)
        t16 = cv.take([128, 2, 16], F32)
        ytmp_m = cv.take([128, 2, 1024], F32)

        B = {}

        def buf(name):
            if name not in B:
                B[name] = Buf(name)
            return B[name]

        stat_ctr = [0]

        def stat():
            i = stat_ctr[0] % 64
            stat_ctr[0] += 1
            return stats[:, i:i + 1], buf("stat%d" % i)

        xld_ctr = [0]
        xs_ctr = [0]

        def xslot():
            i = xld_ctr[0] % 4
            xld_ctr[0] += 1
            return i, buf("xld%d" % i)

        sch.add("sp", I("dma_start", out=cst[:], in_=consts), writes=[buf("cst")], dma="misc0")
        for nm, t, c0 in (("ident_b", ident_b, 0), ("uneg_b", uneg_b, 128), ("nones_b", nones_b, 256), ("nmask_b", nmask_b, 384)):
            sch.add("dve", I("tensor_copy", out=t[:], in_=cst[:, c0:c0 + 128]),
                    reads=[buf("cst")], writes=[buf(nm)])
        ng = norm_g.rearrange("l j (c p) -> (l j c) p", p=128)
        cwv = conv_w.rearrange("l k (c p) -> (l k c) p", p=128)
        psv = pool_scale.rearrange("l (c p) -> (l c) p", p=128)
        sch.add("sp", I("dma_start", out=pstage[:, 0, :], in_=ng[0:128]), writes=[buf("pstage")], dma="misc1")
        sch.add("sp", I("dma_start", out=pstage[0:64, 1, :], in_=ng[128:192]), writes=[buf("pstage")], dma="misc2")
        sch.add("sp", I("dma_start", out=pstage[64:88, 1, :], in_=cwv), writes=[buf("pstage")], dma="misc3")
        sch.add("sp", I("dma_start", out=pstage[88:96, 1, :], in_=psv), writes=[buf("pstage")], dma="misc4")
        sch.add("pe", I("transpose", out=banks[0][:, 0:128], in_=pstage[:, 0, :], identity=ident_f),
                reads=[buf("pstage"), buf("cst")], writes=[bankb[0]])
        sch.add("pe", I("transpose", out=banks[0][:, 128:224], in_=pstage[0:96, 1, :], identity=cst[0:96, 0:96]),
                reads=[buf("pstage"), buf("cst")], writes=[bankb[0]])
        sch.add("dve", I("tensor_copy", out=paramT[:], in_=banks[0][:, 0:224]), reads=[bankb[0]], writes=[buf("paramT")])
        sch.add("pool", I("memset", bdstage[:], 0.0), writes=[buf("bdstage")])
        pwv = pool_w.rearrange("l (ch gp) c d -> gp c (l ch) d", gp=2)
        for gp in range(2):
            sch.add("sp", I("dma_start", out=bdstage[gp * 64:(gp + 1) * 64, :, gp * 64:(gp + 1) * 64], in_=pwv[gp]),
                    writes=[buf("bdstage")], dma="misc%d" % (5 + gp))
        sch.add("dve", I("tensor_copy", out=BD[:], in_=bdstage[:]), reads=[buf("bdstage")], writes=[buf("BD")])

        def phase_units(kind, l, j, step):
            us = []
            if kind == "ffn":
                for knd, W in (("g", w_gate), ("u", w_up)):
                    for u0 in range(0, NF, step):
                        n = min(step, NF - u0)
                        us.append(("col", W[l, j], u0, n, chunk_id(knd, l, j, u0)))
                for u0 in range(0, NF, step):
                    n = min(step, NF - u0)
                    us.append(("row", w_down[l, j], u0, n, chunk_id("d", l, j, u0)))
            else:
                for u0 in range(0, 20, step):
                    us.append(("col", w_in[l], u0, min(step, 20 - u0), chunk_id("wi", l, 0, u0)))
                for u0 in range(0, 8, step):
                    us.append(("row", w_out[l], u0, min(step, 8 - u0), chunk_id("wo", l, 0, u0)))
            return us

        units = phase_units(*phases[0], 4)
        wscb = {}
        cast_engs = ("dve", "act")
        def unit_views(ui):
            typ, W, u0, n, cid = units[ui]
            sf = stg_f[ui % 2]
            sbf = stg_b[ui % 2]
            if typ == "col":
                src = W[:, u0 * 128:(u0 + n) * 128].rearrange("(kc p) c -> p kc c", p=128)
                dstv = sf[:, 0:8 * n * 128].rearrange("p (kc c) -> p kc c", kc=8)
                cin = sf[:, 0:8 * n * 128].rearrange("p (kc j c) -> p j kc c", kc=8, j=n)
                cout = sbf[:, 0:n * 1024].rearrange("p (j kc c) -> p j kc c", j=n, kc=8)
            else:
                src = W[u0 * 128:(u0 + n) * 128, :].rearrange("(f p) c -> p f c", p=128)
                dstv = sf[:, 0:n * 1024].rearrange("p (f c) -> p f c", f=n)
                cin = sf[:, 0:n * 1024]
                cout = sbf[:, 0:n * 1024]
            return src, dstv, cin, cout, sbf, n, cid

        def unit_load(ui):
            src, dstv, cin, cout, sbf, n, cid = unit_views(ui)
            sch.add("sp", I("dma_start", out=dstv, in_=src), writes=[buf("stgf%d" % (ui % 2))], dma="pl%d" % (ui % 2))

        if units:
            unit_load(0)
        for ui in range(len(units)):
            if ui + 1 < len(units):
                unit_load(ui + 1)
            src, dstv, cin, cout, sbf, n, cid = unit_views(ui)
            bf_ = buf("stgf%d" % (ui % 2))
            bb_ = buf("stgb%d" % (ui % 2))
            ce = cast_engs[ui % 2]
            if ce == "act":
                sch.add("act", I("activation", out=cout, in_=cin, func=AF.Copy), reads=[bf_], writes=[bb_])
            else:
                sch.add(ce, I("tensor_copy", out=cout, in_=cin), reads=[bf_], writes=[bb_])
            wb = buf("wsc%d" % cid)
            for k in range(n):
                wscb[cid + k] = wb
            sch.add("sp", I("dma_start", out=wsc[:, cid:cid + n, :], in_=sbf[:, 0:n * 1024].rearrange("p (j c) -> p j c", j=n)),
                    reads=[bb_], writes=[wb], dma="ps%d" % (ui % 2))
        sch.barrier()

        order = order_in
        rec_order = []
        ring_pos = [0]
        ringb = [buf("ring%d" % s_) for s_ in range(NSLOT)]

        def ring_issue(k):
            if order is None or k >= len(order):
                return
            kind, l, j, idx = order[k]
            cid = chunk_id(kind, l, j, idx)
            s_ = k % NSLOT
            sch.add("sp", I("dma_start", out=ring[:, s_, :], in_=wsc[:, cid, :]),
                    reads=[wscb[cid]], writes=[ringb[s_]], dma="r%d" % s_)

        for k in range(NSLOT):
            ring_issue(k)

        def ring_acquire(key):
            k = ring_pos[0]
            rec_order.append(key)
            if order is not None:
                assert order[k] == key, (order[k], key)
            s_ = k % NSLOT
            return ring[:, s_, :], ringb[s_]

        def ring_release():
            k = ring_pos[0]
            ring_pos[0] += 1
            ring_issue(k + NSLOT)

        def gT_cols(l, j):
            o = (l * 6 + j) * 8
            return paramT[:, o:o + 8]

        def prenorm_tile(src, row0, l, jn, xnT, col0, tbank):
            prenorm_b(prenorm_a(src, row0, l, jn, xnT, col0, tbank))

        def prenorm_a(src, row0, l, jn, xnT, col0, tbank, ldq="sp"):
            si, sbuf_ = xslot()
            sch.add(ldq, I("dma_start", out=xld[:, si, :], in_=src[row0:row0 + 128, :]),
                    reads=[buf("xd%d" % (row0 // 128))], writes=[sbuf_], dma=("xl%d" if ldq == "sp" else "xp%d") % si)
            ssq, ssqb = stat()
            rms, rmsb = stat()
            rstd, rstdb = stat()
            sch.add("act", I("activation", out=junk[:], in_=xld[:, si, :], func=AF.Square, accum_out=ssq),
                    reads=[sbuf_], writes=[buf("junk"), ssqb])
            sch.add("pool", I("tensor_scalar", out=rms, in0=ssq, scalar1=1.0 / D, scalar2=EPS, op0=ALU.mult, op1=ALU.add),
                    reads=[ssqb], writes=[rmsb])
            sch.add("pool", I("tensor_tensor", out=rstd, in0=rms, in1=cst[:, 546:547], op=ALU.pow), reads=[rmsb, buf("cst")], writes=[rstdb])
            xs_ctr[0] += 1
            par = xs_ctr[0] % 2
            xsb = buf("xs%d" % par)
            sch.add("dve", I("tensor_scalar", out=xs[:, par, :], in0=xld[:, si, :], scalar1=rstd, scalar2=None, op0=ALU.mult),
                    reads=[sbuf_, rstdb], writes=[xsb])
            return (par, xsb, l, jn, xnT, col0, tbank)

        def prenorm_b(state):
            par, xsb, l, jn, xnT, col0, tbank = state
            pT = banks[tbank][:].bitcast(BF16)
            for c in range(8):
                sch.add("pe", I("transpose", out=pT[:, c * 128:(c + 1) * 128], in_=xs[:, par, c * 128:(c + 1) * 128],
                                                          identity=ident_b[:]),
                        reads=[xsb, buf("ident_b")], writes=[bankb[tbank]])
            g8 = gT_cols(l, jn)
            sch.add("dve", I("tensor_tensor", out=xnT[:, :, col0:col0 + 128], in0=pT.rearrange("p (c t) -> p c t", c=8),
                                                     in1=g8.unsqueeze(2).to_broadcast([128, 8, 128]), op=ALU.mult),
                    reads=[bankb[tbank], buf("paramT")], writes=[buf("xnT")])

        def postnorm_tile(pb0, pb1, bb0, bb1, src, dst, row0, gpar, half_scale, ytile, ytb, is_last_phase, ldq="sp"):
            si, sbuf_ = xslot()
            sch.add(ldq, I("dma_start", out=xld[:, si, :], in_=src[row0:row0 + 128, :]),
                    reads=[buf("xd%d" % (row0 // 128))], writes=[sbuf_], dma=("xl%d" if ldq == "sp" else "xp%d") % si)
            q0, q0b = stat()
            q1, q1b = stat()
            qs, qsb = stat()
            rms, rmsb = stat()
            rstd, rstdb = stat()
            sch.add("act", I("activation", out=junk[:, 0:512], in_=pb0[:], func=AF.Square, accum_out=q0),
                    reads=[bb0], writes=[buf("junk"), q0b])
            sch.add("act", I("activation", out=junk[:, 512:1024], in_=pb1[:], func=AF.Square, accum_out=q1),
                    reads=[bb1], writes=[buf("junk"), q1b])
            sch.add("pool", I("tensor_tensor", out=qs, in0=q0, in1=q1, op=ALU.add), reads=[q0b, q1b], writes=[qsb])
            k = 4.0 if half_scale else 1.0
            sch.add("pool", I("tensor_scalar", out=rms, in0=qs, scalar1=k / D, scalar2=k * EPS, op0=ALU.mult, op1=ALU.add),
                    reads=[qsb], writes=[rmsb])
            sch.add("pool", I("tensor_tensor", out=rstd, in0=rms, in1=cst[:, 546:547], op=ALU.pow), reads=[rmsb, buf("cst")], writes=[rstdb])
            for dh, (pb, bb) in enumerate(((pb0, bb0), (pb1, bb1))):
                sch.add("dve", I("scalar_tensor_tensor",
                    out=ytile[:, dh * 512:(dh + 1) * 512], in0=pb[:], scalar=rstd, in1=gbc[:, gpar, dh * 512:(dh + 1) * 512],
                    op0=ALU.mult, op1=ALU.mult), reads=[bb, rstdb, buf("gbc%d" % gpar)], writes=[ytb])
            sch.add("pool", I("tensor_tensor", out=xld[:, si, :], in0=ytile, in1=xld[:, si, :], op=ALU.add),
                    reads=[ytb, sbuf_], writes=[sbuf_])
            sch.add("pool", I("dma_start", out=dst[row0:row0 + 128, :], in_=xld[:, si, :]),
                    reads=[sbuf_], writes=[buf("xd%d" % (row0 // 128))], dma="st%d" % si)

        jit = {"pending": [], "loaded": None, "cast": None, "k": 0}

        def jit_views(u):
            typ, W, u0, n, cid, k = u
            sf = jstg_f[k % 2]
            sbf = jstg_b[k % 2]
            if typ == "col":
                src_ = W[:, u0 * 128:(u0 + n) * 128].rearrange("(kc p) c -> p kc c", p=128)
                dstv = sf[:, 0:8 * n * 128].rearrange("p (kc c) -> p kc c", kc=8)
                cin = sf[:, 0:8 * n * 128].rearrange("p (kc j c) -> p j kc c", kc=8, j=n)
                cout = sbf[:, 0:n * 1024].rearrange("p (j kc c) -> p j kc c", j=n, kc=8)
            else:
                src_ = W[u0 * 128:(u0 + n) * 128, :].rearrange("(f p) c -> p f c", p=128)
                dstv = sf[:, 0:n * 1024].rearrange("p (f c) -> p f c", f=n)
                cin = sf[:, 0:n * 1024]
                cout = sbf[:, 0:n * 1024]
            return src_, dstv, cin, cout, sbf

        def jit_step():
            if jit["cast"] is not None:
                u = jit["cast"]
                typ, W, u0, n, cid, k = u
                src_, dstv, cin, cout, sbf = jit_views(u)
                sch.add("sp", I("dma_start", out=wsc[:, cid:cid + n, :], in_=sbf[:, 0:n * 1024].rearrange("p (j c) -> p j c", j=n)),
                        reads=[buf("jstgb%d" % (k % 2))], writes=[wscb[cid]], dma="js%d" % (k % 2))
                jit["cast"] = None
            if jit["loaded"] is not None:
                u = jit["loaded"]
                typ, W, u0, n, cid, k = u
                src_, dstv, cin, cout, sbf = jit_views(u)
                if k % 2 == 0:
                    sch.add("dve", I("tensor_copy", out=cout, in_=cin), reads=[buf("jstgf%d" % (k % 2))], writes=[buf("jstgb%d" % (k % 2))])
                else:
                    sch.add("act", I("activation", out=cout, in_=cin, func=AF.Copy), reads=[buf("jstgf%d" % (k % 2))], writes=[buf("jstgb%d" % (k % 2))])
                jit["cast"] = u
                jit["loaded"] = None
            if jit["pending"]:
                u = jit["pending"].pop(0)
                typ, W, u0, n, cid, k = u
                src_, dstv, cin, cout, sbf = jit_views(u)
                sch.add("sp", I("dma_start", out=dstv, in_=src_), writes=[buf("jstgf%d" % (k % 2))], dma="jl%d" % (k % 2))
                jit["loaded"] = u

        def jit_busy():
            return bool(jit["pending"]) or jit["loaded"] is not None or jit["cast"] is not None

        def jit_enqueue(pi):
            for pj in range(pi + 1, len(phases)):
                for (typ, W, u0, n, cid) in phase_units(*phases[pj], 2):
                    wb = buf("wsc%d" % cid)
                    for k_ in range(n):
                        wscb[cid + k_] = wb
                    jit["pending"].append((typ, W, u0, n, cid, jit["k"]))
                    jit["k"] += 1
                if phases[pj][0] == "ffn":
                    break

        pre_done = [False]
        for pi, (kind, l, j) in enumerate(phases):
            src = x_in if pi == 0 else xbuf
            dst = y_out if pi == len(phases) - 1 else xbuf
            gpar = pi % 2
            jpre, jpost = (4 * j, 4 * j + 1) if kind == "ffn" else (2, 3)
            sch.add("sp", I("dma_start",
                out=gbc[:, gpar, :], in_=norm_g[l, jpost:jpost + 1, :].partition_broadcast(128)),
                writes=[buf("gbc%d" % gpar)], dma="gb%d" % gpar)
            if kind == "ffn":
                jit_enqueue(pi)
                cd = chunk_id("d", l, j, 0)
                sch.add("sp", I("dma_start", out=Wd[:, 0:11, :], in_=wsc[:, cd:cd + 11, :]),
                        reads=[wscb[cd + k] for k in range(11)], writes=[buf("Wd")], dma="wd0")
                sch.add("sp", I("dma_start", out=Wd[:, 11:22, :], in_=wsc[:, cd + 11:cd + 22, :]),
                        reads=[wscb[cd + k] for k in range(11, 22)], writes=[buf("Wd")], dma="wd1")
                for G in range(4):
                    if G == 0 and not pre_done[0]:
                        for tt in range(8):
                            prenorm_tile(src, tt * 128, l, jpre, xnT_f, tt * 128, tt % 4)
                    pre_done[0] = False
                    nxt = None
                    if G < 3:
                        nxt = (src, (G + 1) * 1024, l, jpre)
                    elif pi + 1 < len(phases) and phases[pi + 1][0] == "ffn":
                        nxt = (xbuf, 0, phases[pi + 1][1], 4 * phases[pi + 1][2])
                        pre_done[0] = True
                    for f in range(NF):
                        jit_step()
                        wg, wgb = ring_acquire(("g", l, j, f))
                        wgv = wg.rearrange("p (kc c) -> p kc c", kc=8)
                        for half in range(2):
                            for kc in range(8):
                                sch.add("pe", I("matmul",
                                    banks[half][:], lhsT=wgv[:, kc, :], rhs=xnT_f[:, kc, half * 512:(half + 1) * 512],
                                    start=(kc == 0), stop=(kc == 7)), reads=[wgb, buf("xnT")], writes=[bankb[half]])
                        ring_release()
                        wu, wub = ring_acquire(("u", l, j, f))
                        wuv = wu.rearrange("p (kc c) -> p kc c", kc=8)
                        for half in range(2):
                            for kc in range(8):
                                sch.add("pe", I("matmul",
                                    banks[2 + half][:], lhsT=wuv[:, kc, :], rhs=xnT_f[:, kc, half * 512:(half + 1) * 512],
                                    start=(kc == 0), stop=(kc == 7)), reads=[wub, buf("xnT")], writes=[bankb[2 + half]])
                        ring_release()
                        for half in range(2):
                            sch.add("act", I("activation", out=sg[:, half, :], in_=banks[half][:], func=AF.Silu),
                                    reads=[bankb[half]], writes=[buf("sg%d" % half)])
                            sch.add("dve", I("tensor_tensor",
                                out=hT[:, f, half * 512:(half + 1) * 512], in0=sg[:, half, :], in1=banks[2 + half][:], op=ALU.mult),
                                reads=[buf("sg%d" % half), bankb[2 + half]], writes=[buf("hT%d" % f)])
                    for tt in range(8):
                        b0 = 4 + 2 * (tt % 2)
                        pst = None
                        if nxt is not None:
                            pst = prenorm_a(nxt[0], nxt[1] + tt * 128, nxt[2], nxt[3], xnT_f, tt * 128, tt % 4)
                        for dh in range(2):
                            for f in range(NF):
                                sch.add("pe", I("matmul",
                                    banks[b0 + dh][:], lhsT=hT[:, f, tt * 128:(tt + 1) * 128], rhs=Wd[:, f, dh * 512:(dh + 1) * 512],
                                    start=(f == 0), stop=(f == NF - 1)), reads=[buf("hT%d" % f), buf("Wd")], writes=[bankb[b0 + dh]])
                        if pst is not None:
                            prenorm_b(pst)
                        postnorm_tile(banks[b0], banks[b0 + 1], bankb[b0], bankb[b0 + 1], src, dst, G * 1024 + tt * 128,
                                      gpar, True, ytmp[:, tt % 2, :], buf("ytmp%d" % (tt % 2)), False)
                while jit_busy():
                    jit_step()
            else:
                cwcol = lambda tap, ch: paramT[:, 192 + (l * 3 + tap) * 2 + ch:192 + (l * 3 + tap) * 2 + ch + 1]
                pscol = lambda ch: paramT[:, 216 + l * 2 + ch:216 + l * 2 + ch + 1]
                sch.add("pool", I("memset", ubuf[:, :, 0:2], 0.0), writes=[buf("ubuf")])
                sch.add("pool", I("memset", pbuf[:, :, 0:16], 0.0), writes=[buf("pbuf")])
                sch.add("pool", I("memset", qT[:], 0.0), writes=[buf("qT%d" % c_) for c_ in range(4)])
                bgb = [0]

                def nextbank():
                    bgb[0] += 1
                    return 6 + bgb[0] % 2

                def proj_fm(ch, bank):
                    w, wb = ring_acquire(("wi", l, 0, ch))
                    wv = w.rearrange("p (kc c) -> p kc c", kc=8)
                    for kc in range(8):
                        sch.add("pe", I("matmul", banks[bank][:], lhsT=wv[:, kc, :], rhs=xnT_m[:, kc, :], start=(kc == 0), stop=(kc == 7)),
                                reads=[wb, buf("xnT")], writes=[bankb[bank]])
                    ring_release()

                def t_pre(i):
                    stt = {}

                    def mk(tt):
                        def f():
                            if tt < 4:
                                stt[tt] = prenorm_a(src, i * 512 + tt * 128, l, jpre, xnT_m, tt * 128, 6 + tt % 2, ldq="pool")
                            if tt >= 1:
                                prenorm_b(stt[tt - 1])
                        return f
                    return [mk(tt) for tt in range(5)]

                def t_q(i, c):
                    def f():
                        bk = nextbank()
                        proj_fm(c, bk)
                        for hh in range(2):
                            hp_ = slice(hh * 64, hh * 64 + 64)
                            sch.add("dve", I("tensor_scalar", out=qT[hp_, c, hh, :], in0=banks[bk][hp_, :], scalar1=0.125, scalar2=None, op0=ALU.mult),
                                    reads=[bankb[bk]], writes=[buf("qT%d" % c)])
                    return [f]

                def t_k(i, c):
                    def f():
                        bk = nextbank()
                        proj_fm(4 + c, bk)
                        sch.add("dve", I("tensor_copy", out=kT[:, c, i * 512:(i + 1) * 512], in_=banks[bk][:]),
                                reads=[bankb[bk]], writes=[buf("kT%d_%d" % (c, i))])
                    return [f]

                def t_v(i, jv):
                    def f():
                        bk = nextbank()
                        w, wb = ring_acquire(("wi", l, 0, 8 + jv))
                        wv = w.rearrange("p (kc c) -> p kc c", kc=8)
                        for tt in range(4):
                            for kc in range(8):
                                sch.add("pe", I("matmul", banks[bk][:, tt * 128:(tt + 1) * 128], lhsT=xnT_m[:, kc, tt * 128:(tt + 1) * 128], rhs=wv[:, kc, :],
                                                start=(kc == 0), stop=(kc == 7)), reads=[wb, buf("xnT")], writes=[bankb[bk]])
                        ring_release()
                        sch.add("dve", I("tensor_copy", out=vS[:, 4 * i:4 * i + 4, jv * 128:(jv + 1) * 128], in_=banks[bk][:].rearrange("p (t c) -> p t c", t=4)),
                                reads=[bankb[bk]], writes=[buf("v%d_%d" % (i, jv))])
                    return [f]

                def t_convpool(i):
                    sl = []

                    def mk_bc(ch, which):
                        def f():
                            bk = nextbank()
                            proj_fm((12 if which == "B" else 14) + ch, bk)
                            dst_ = Bsb if which == "B" else Csb
                            sch.add("dve", I("tensor_copy", out=dst_[:, ch, :], in_=banks[bk][:]),
                                    reads=[bankb[bk]], writes=[buf("%ssb%d" % (which, ch))])
                        return f

                    def mk_h(ch):
                        def f():
                            bk = nextbank()
                            proj_fm(16 + ch, bk)
                            ub = buf("ubuf%d" % ch)
                            yb = buf("ycv%d" % ch)
                            sch.add("dve", I("tensor_tensor", out=ubuf[:, ch, 2:514], in0=banks[bk][:], in1=Csb[:, ch, :], op=ALU.mult),
                                    reads=[bankb[bk], buf("Csb%d" % ch), buf("ubuf")], writes=[ub])
                            sch.add("dve", I("tensor_scalar", out=ycv[:, ch, :], in0=ubuf[:, ch, 2:514], scalar1=cwcol(2, ch), scalar2=None, op0=ALU.mult),
                                    reads=[ub, buf("paramT")], writes=[yb])
                            sch.add("dve", I("scalar_tensor_tensor", out=ycv[:, ch, :], in0=ubuf[:, ch, 1:513], scalar=cwcol(1, ch), in1=ycv[:, ch, :],
                                             op0=ALU.mult, op1=ALU.add), reads=[ub, buf("paramT"), yb], writes=[yb])
                            sch.add("dve", I("scalar_tensor_tensor", out=ycv[:, ch, :], in0=ubuf[:, ch, 0:512], scalar=cwcol(0, ch), in1=ycv[:, ch, :],
                                             op0=ALU.mult, op1=ALU.add), reads=[ub, buf("paramT"), yb], writes=[yb])
                            sch.add("dve", I("tensor_tensor", out=mixT[:, 4 + ch, :], in0=ycv[:, ch, :], in1=Bsb[:, ch, :], op=ALU.mult),
                                    reads=[yb, buf("Bsb%d" % ch)], writes=[buf("mixT_%d" % (4 + ch))])
                            sch.add("dve", I("tensor_copy", out=ubuf[:, ch, 0:2], in_=ubuf[:, ch, 512:514]), reads=[ub], writes=[ub])
                        return f

                    def mk_p(ch):
                        def f():
                            bk = nextbank()
                            proj_fm(18 + ch, bk)
                            sch.add("dve", I("tensor_copy", out=pbuf[:, ch, 16:528], in_=banks[bk][:]),
                                    reads=[bankb[bk]], writes=[buf("pbuf")])
                        return f

                    def f_pool():
                        pb_ = buf("pbuf")
                        sb_ = buf("pools")
                        sch.add("pool", I("tensor_tensor", out=s2[:, :, 1:528], in0=pbuf[:, :, 1:528], in1=pbuf[:, :, 0:527], op=ALU.add),
                                reads=[pb_], writes=[sb_])
                        sch.add("pool", I("tensor_tensor", out=s4[:, :, 3:528], in0=s2[:, :, 3:528], in1=s2[:, :, 1:526], op=ALU.add),
                                reads=[sb_], writes=[buf("pools4")])
                        sch.add("pool", I("tensor_tensor", out=s8[:, 7:528], in0=s4[:, 1, 7:528], in1=s4[:, 1, 3:524], op=ALU.add),
                                reads=[buf("pools4")], writes=[buf("pools8")])
                        sch.add("pool", I("tensor_tensor", out=s16[:, 15:528], in0=s8[:, 15:528], in1=s8[:, 7:520], op=ALU.add),
                                reads=[buf("pools8")], writes=[buf("pools16")])
                        dTb = buf("dT")
                        sels = ((0, 0, s2[0:64, 0, :], sb_), (0, 1, s4[64:128, 0, :], buf("pools4")),
                                (1, 0, s8[0:64, :], buf("pools8")), (1, 1, s16[64:128, :], buf("pools16")))
                        for ch, gp, sv, svb in sels:
                            pr = slice(gp * 64, gp * 64 + 64)
                            sch.add("dve", I("scalar_tensor_tensor", out=dT[pr, ch, :], in0=sv[:, 16:528], scalar=cst[pr, 544 + ch:545 + ch], in1=pbuf[pr, ch, 16:528],
                                             op0=ALU.mult, op1=ALU.subtract), reads=[svb, pb_, buf("cst")], writes=[dTb])
                            if i == 0:
                                sch.add("dve", I("tensor_tensor", out=t16[pr, ch, :], in0=sv[:, 16:32], in1=cst[pr, 512 + ch * 16:528 + ch * 16], op=ALU.mult),
                                        reads=[svb, buf("cst")], writes=[buf("t16")])
                                sch.add("dve", I("tensor_tensor", out=dT[pr, ch, 0:16], in0=t16[pr, ch, :], in1=pbuf[pr, ch, 16:32], op=ALU.subtract),
                                        reads=[buf("t16"), pb_], writes=[dTb])
                        for ch in range(2):
                            bk = nextbank()
                            sch.add("pe", I("matmul", banks[bk][:], lhsT=BD[:, l * 2 + ch, :], rhs=dT[:, ch, :], start=True, stop=True),
                                    reads=[buf("BD"), dTb], writes=[bankb[bk]])
                            sch.add("dve", I("tensor_scalar", out=mixT[:, 6 + ch, :], in0=banks[bk][:], scalar1=pscol(ch), scalar2=None, op0=ALU.mult),
                                    reads=[bankb[bk], buf("paramT")], writes=[buf("mixT_%d" % (6 + ch))])
                        sch.add("pool", I("tensor_copy", out=pbuf[:, :, 0:16], in_=pbuf[:, :, 512:528]), reads=[pb_], writes=[pb_])

                    for ch in range(2):
                        sl.append(mk_bc(ch, "B"))
                    for ch in range(2):
                        sl.append(mk_bc(ch, "C"))
                    for ch in range(2):
                        sl.append(mk_h(ch))
                    for ch in range(2):
                        sl.append(mk_p(ch))
                    sl.append(f_pool)
                    return sl

                def t_out(i):
                    sl = []

                    def mk(tt, part):
                        def f():
                            for kc in range(4 * part, 4 * part + 4):
                                w, wb = ring_acquire(("wo", l, 0, kc))
                                if kc < 4:
                                    rds = [buf("mixT_%d_0" % kc), buf("mixT_%d_1" % kc)]
                                else:
                                    rds = [buf("mixT_%d" % kc)]
                                for dh in range(2):
                                    sch.add("pe", I("matmul", banks[6 + dh][:], lhsT=mixT[:, kc, tt * 128:(tt + 1) * 128], rhs=w[:, dh * 512:(dh + 1) * 512],
                                                    start=(kc == 0), stop=(kc == 7)), reads=[wb] + rds, writes=[bankb[6 + dh]])
                                ring_release()
                            if part == 1:
                                postnorm_tile(banks[6], banks[7], bankb[6], bankb[7], src, dst, i * 512 + tt * 128,
                                              gpar, False, ytmp_m[:, tt % 2, :], buf("ytmpm%d" % (tt % 2)), False, ldq="pool")
                        return f
                    for tt in range(4):
                        sl.append(mk(tt, 0))
                        sl.append(mk(tt, 1))
                    return sl

                def attn_stages(i, steps):
                    kb_last = 4 * i + 3

                    def info(n):
                        c, kb, hh = steps[n]
                        r = kb - 4 * i
                        c0 = max(r, 0) * 128
                        return c, kb, hh, r, c0, slice(hh * 64, hh * 64 + 64)

                    def st_z(n):
                        c, kb, hh, r, c0, hp = info(n)
                        zb = n % 2
                        sch.add("pe", I("matmul", banks[zb][:, c0:512], lhsT=kT[:, c, kb * 128:(kb + 1) * 128], rhs=qT[:, c, hh, c0:512],
                                        start=True, stop=(r < 0)),
                                reads=[buf("kT%d_%d" % (c, kb // 4)), buf("qT%d" % c)], writes=[bankb[zb]])
                        if r >= 0:
                            sch.add("pe", I("matmul", banks[zb][:, c0:c0 + 128], lhsT=ident_b[:], rhs=nmask_b[:], start=False, stop=True),
                                    reads=[buf("ident_b"), buf("nmask_b")], writes=[bankb[zb]])

                    def st_e(n):
                        c, kb, hh, r, c0, hp = info(n)
                        zb = n % 2
                        sch.add("act", I("activation", out=banks[zb][:, c0:512], in_=banks[zb][:, c0:512], func=AF.Exp),
                                reads=[bankb[zb]], writes=[bankb[zb]])

                    def st_sp(n):
                        c, kb, hh, r, c0, hp = info(n)
                        zb = n % 2
                        k3 = n % 3
                        sch.add("act", I("activation", out=SPt[:, k3, c0:512], in_=banks[zb][:, c0:512], func=AF.Ln, bias=1.0),
                                reads=[bankb[zb]], writes=[buf("SP%d" % k3)])

                    def st_b(n):
                        c, kb, hh, r, c0, hp = info(n)
                        bbk = 2 + n % 2
                        k3 = n % 3
                        first = (kb == kb_last)
                        rp = (kb_last - kb) % 2
                        Rold, Roldb = Rt[:, hh * 2 + (1 - rp), :], buf("R%d" % (hh * 2 + 1 - rp))
                        Rnew, Rnewb = Rt[:, hh * 2 + rp, :], buf("R%d" % (hh * 2 + rp))
                        sch.add("pe", I("matmul", banks[bbk][:, c0:512], lhsT=kT[:, c, kb * 128:(kb + 1) * 128], rhs=qT[:, c, hh, c0:512],
                                        start=True, stop=False),
                                reads=[buf("kT%d_%d" % (c, kb // 4)), buf("qT%d" % c)], writes=[bankb[bbk]])
                        if r >= 0:
                            sch.add("pe", I("matmul", banks[bbk][:, c0:c0 + 128], lhsT=ident_b[:], rhs=nmask_b[:], start=False, stop=False),
                                    reads=[buf("ident_b"), buf("nmask_b")], writes=[bankb[bbk]])
                        cc = c0 + 128 if r >= 0 else 0
                        if not first and cc < 512:
                            sch.add("pe", I("matmul", banks[bbk][:, cc:512], lhsT=nones_b[:], rhs=Rold[:, cc:512], start=False, stop=False),
                                    reads=[buf("nones_b"), Roldb], writes=[bankb[bbk]])
                        sch.add("pe", I("matmul", banks[bbk][:, c0:512], lhsT=uneg_b[:], rhs=SPt[:, k3, c0:512], start=False, stop=True),
                                reads=[buf("uneg_b"), buf("SP%d" % k3)], writes=[bankb[bbk]])
                        if kb > 0:
                            if r >= 0:
                                sch.add("dve", I("tensor_copy", out=Rnew[:, c0:c0 + 128], in_=SPt[:, k3, c0:c0 + 128]),
                                        reads=[buf("SP%d" % k3)], writes=[Rnewb])
                            if cc < 512 and not first:
                                sch.add("dve", I("tensor_tensor", out=Rnew[:, cc:512], in0=Rold[:, cc:512], in1=SPt[:, k3, cc:512], op=ALU.add),
                                        reads=[buf("SP%d" % k3), Roldb], writes=[Rnewb])

                    def st_a(n):
                        c, kb, hh, r, c0, hp = info(n)
                        bbk = 2 + n % 2
                        k3 = n % 3
                        sch.add("act", I("activation", out=At[:, k3, c0:512], in_=banks[bbk][:, c0:512], func=AF.Exp),
                                reads=[bankb[bbk]], writes=[buf("A%d" % k3)])

                    def st_av(n):
                        c, kb, hh, r, c0, hp = info(n)
                        k3 = n % 3
                        ob = 4 + hh
                        sch.add("pe", I("matmul", banks[ob][:, c0:512], lhsT=vS[:, kb, c * 128:(c + 1) * 128], rhs=At[:, k3, c0:512],
                                        start=(kb == kb_last), stop=(kb == 0), skip_group_check=(kb != kb_last and r >= 0)),
                                reads=[buf("v%d_%d" % (kb // 4, c)), buf("A%d" % k3)], writes=[bankb[ob]])
                        if kb == 0:
                            sch.add("dve", I("tensor_copy", out=mixT[hp, c, :], in_=banks[ob][hp, :]),
                                    reads=[bankb[ob]], writes=[buf("mixT_%d_%d" % (c, hh))])
                    return st_z, st_e, st_sp, st_b, st_a, st_av

                for s_ in t_pre(0):
                    s_()
                for c in range(4):
                    for s_ in t_q(0, c) + t_k(0, c) + t_v(0, c):
                        s_()
                for s_ in t_convpool(0):
                    s_()

                for i in range(8):
                    kb_last = 4 * i + 3
                    steps = []
                    for c in range(4):
                        for kb in range(kb_last, -1, -1):
                            for hh in range(2):
                                steps.append((c, kb, hh))
                    NS = len(steps)
                    P = NS // 4
                    st_z, st_e, st_sp, st_b, st_a, st_av = attn_stages(i, steps)
                    tasks = []
                    if i > 0:
                        tasks += [(0, "q3", s_) for s_ in t_q(i, 3)]
                        tasks += [(0, "out", s_) for s_ in t_out(i - 1)]
                        tasks += [(0, "cp", s_) for s_ in t_convpool(i)]
                    if i < 7:
                        tasks += [(0, "pre", s_) for s_ in t_pre(i + 1)]
                        for c in range(4):
                            tasks += [(0, "kv", s_) for s_ in t_k(i + 1, c) + t_v(i + 1, c)]
                        for c in range(3):
                            tasks += [((c + 1) * P + 3, "q", s_) for s_ in t_q(i + 1, c)]
                    ntask = len(tasks)
                    pulled = [0]

                    def flush_tag(tag):
                        last = -1
                        for ti, (rd, tg, fn_) in enumerate(tasks):
                            if tg == tag:
                                last = ti
                        while pulled[0] <= last:
                            tasks[pulled[0]][2]()
                            pulled[0] += 1

                    for it in range(NS + 4):
                        if i > 0 and it == P:
                            flush_tag("out")
                        if i > 0 and it == 3 * P:
                            flush_tag("q3")
                        if it < NS:
                            st_z(it)
                        if 0 <= it - 1 < NS:
                            st_e(it - 1)
                        if 0 <= it - 3 < NS:
                            st_a(it - 3)
                        if 0 <= it - 1 < NS:
                            st_sp(it - 1)
                        if 0 <= it - 2 < NS:
                            st_b(it - 2)
                        if 0 <= it - 4 < NS:
                            st_av(it - 4)
                        target = (it + 1) * ntask // NS
                        while pulled[0] < min(target, ntask) and tasks[pulled[0]][0] <= it:
                            tasks[pulled[0]][2]()
                            pulled[0] += 1
                    while pulled[0] < ntask:
                        tasks[pulled[0]][2]()
                        pulled[0] += 1
                for s_ in t_out(7):
                    s_()
            if not (kind == "ffn" and pi + 1 < len(phases) and phases[pi + 1][0] == "ffn"):
                sch.barrier()

        if order_in is None:
            return rec_order
        sch.emit(nc, st)
    return nc


_CACHE = {}


def run(inputs, nphases=12):
    if nphases not in _CACHE:
        _CACHE[nphases] = build_program(nphases)
    nc = _CACHE[nphases]
    cst = make_consts()
    shared = {k: np.ascontiguousarray(np.asarray(inputs[k], dtype=np.float32)) for k in
              ("norm_g", "ffn_w_gate", "ffn_w_up", "ffn_w_down", "w_in", "conv_w", "pool_w", "pool_scale", "w_out")}
    x = np.asarray(inputs["x"], dtype=np.float32)
    in_maps = []
    for b in range(NCORES):
        m = dict(shared)
        m["x"] = np.ascontiguousarray(x[b])
        m["consts"] = cst
        in_maps.append(m)
    res = run_bass_kernel_spmd(nc, in_maps, core_ids=list(range(NCORES)))
    return np.stack([np.asarray(r["y"], dtype=np.float32) for r in res.results], axis=0)


def kernel(**inputs):
    return run(inputs, 12)
```
